# Optimizing a Trainium2 kernel written in Bass

```python
import math
import jax, jax.numpy as jnp
from jax import lax
import numpy as np

D_MODEL = 1024
BATCH = 8
SEQ = 2048
DEPTH = 4


CTX_LEN = 256
GRID_W = 64
N_MIXERS = 4
N_MOD = 9
D_FF = 2816
EPS = 1e-6
NEG = -1e30

NA_HEADS = 16
NA_HD = D_MODEL // NA_HEADS
NA_WIN_R = 8
NA_WIN_C = 16
NA_QBLK_C = 16
NA_KBLK_C = NA_QBLK_C + NA_WIN_C

ML_HEADS = 4
ML_HD = D_MODEL // ML_HEADS
ML_CONV = 3
ML_CHUNK = 64

DA_HEADS = 8
DA_HD = D_MODEL // DA_HEADS // 2
DA_QBLK = 128
ROPE_BASE = 10000.0

GLA_HEADS = 4
GLA_DK = D_MODEL // 2 // GLA_HEADS
GLA_DV = D_MODEL // GLA_HEADS
GLA_RANK = 16
GLA_TAU = 16.0
GLA_CHUNK = 64

LAYERS_PER_MIXER = tuple((DEPTH - m + N_MIXERS - 1) // N_MIXERS for m in range(N_MIXERS))

kernel_name = 'hybrid_interleaved_diffusion_block'


def rmsnorm(x, g):
    xf = x.astype(jnp.float32)
    y = xf * lax.rsqrt(jnp.mean(xf * xf, axis=-1, keepdims=True) + EPS)
    return (y * g.astype(jnp.float32)).astype(x.dtype)


def head_rmsnorm(o, g):
    of = o.astype(jnp.float32)
    y = of * lax.rsqrt(jnp.mean(of * of, axis=-1, keepdims=True) + EPS)
    return (y * g.reshape(o.shape[-2:]).astype(jnp.float32)).astype(o.dtype)


def modulate(x, shift, scale):
    return x * (1.0 + scale) + shift


def swiglu(x, w13, w2):
    gate, up = jnp.split(x @ w13, 2, axis=-1)
    return (jax.nn.silu(gate) * up) @ w2


def ffn_branch(h, g, shift, scale, gate, w13, w2):
    return 0.5 * gate * swiglu(modulate(rmsnorm(h, g), shift, scale), w13, w2)


def dwconv_centred(x, w):
    K, C = w.shape
    return lax.conv_general_dilated(x, w[:, None, :].astype(x.dtype), (1,), [(K // 2, K // 2)],
                                    dimension_numbers=('NWC', 'WIO', 'NWC'), feature_group_count=C)


def axial_rope_tables(n, dim):
    t = jnp.arange(n)
    row = (t // GRID_W).astype(jnp.float32)
    col = (t % GRID_W).astype(jnp.float32)
    per_axis = dim // 2
    freqs = ROPE_BASE ** (-jnp.arange(0, per_axis, 2, dtype=jnp.float32) / per_axis)
    ar = row[:, None] * freqs
    ac = col[:, None] * freqs
    ang = jnp.concatenate([ar, ar, ac, ac], axis=-1)
    return jnp.cos(ang), jnp.sin(ang)


def apply_axial_rope(x, cos, sin):
    x1, x2, x3, x4 = jnp.split(x, 4, axis=-1)
    rot = jnp.concatenate([-x2, x1, -x4, x3], axis=-1)
    return (x * cos + rot * sin).astype(x.dtype)


def softmax_attend(q, k, v):
    s = jnp.einsum('bthd,bshd->bhts', q, k).astype(jnp.float32)
    p = jax.nn.softmax(s, axis=-1).astype(v.dtype)
    return jnp.einsum('bhts,bshd->bthd', p, v)


def to_chunks(t, size):
    nc = t.shape[2] // size
    return jnp.moveaxis(t.reshape(t.shape[:2] + (nc, size) + t.shape[3:]), 2, 0)


def from_chunks(t):
    t = jnp.moveaxis(t, 0, 2)
    return t.reshape(t.shape[0], t.shape[1], -1, t.shape[-1])


def mlstm_scan(q, k, v, ig, lf, state):
    tril = np.tril(np.ones((ML_CHUNK, ML_CHUNK), bool))

    def step(carry, inp):
        C, n, m = carry
        qc, kc, vc, ic, fc = inp
        b = jnp.cumsum(fc, axis=-1)
        dmat = jnp.where(tril, b[..., :, None] - b[..., None, :] + ic[..., None, :], NEG)
        inter = b + m[..., None]
        m_t = jnp.maximum(inter, jnp.max(dmat, axis=-1))
        w_inter = jnp.exp(inter - m_t)
        s = jnp.einsum('bhtd,bhsd->bhts', qc, kc) * jnp.exp(dmat - m_t[..., None])
        num = w_inter[..., None] * jnp.einsum('bhtd,bhde->bhte', qc, C) + jnp.einsum('bhts,bhse->bhte', s, vc)
        den = w_inter * jnp.einsum('bhtd,bhd->bht', qc, n) + jnp.sum(s, axis=-1)
        h = num / jnp.maximum(jnp.abs(den), jnp.exp(-m_t))[..., None]
        g = b[..., -1:] - b + ic
        m_new = jnp.maximum(b[..., -1] + m, jnp.max(g, axis=-1))
        w_old = jnp.exp(b[..., -1] + m - m_new)
        w_s = jnp.exp(g - m_new[..., None])
        C_new = w_old[..., None, None] * C + jnp.einsum('bhs,bhsd,bhse->bhde', w_s, kc, vc)
        n_new = w_old[..., None] * n + jnp.einsum('bhs,bhsd->bhd', w_s, kc)
        return (C_new, n_new, m_new), h

    final, h = lax.scan(step, state, (to_chunks(q, ML_CHUNK), to_chunks(k, ML_CHUNK), to_chunks(v, ML_CHUNK),
                                      to_chunks(ig, ML_CHUNK), to_chunks(lf, ML_CHUNK)))
    return from_chunks(h), final


def mlstm_final_state(_q, k, v, ig, lf):
    b = jnp.cumsum(lf, axis=-1)
    g = b[..., -1:] - b + ig
    m = jnp.max(g, axis=-1)
    w = jnp.exp(g - m[..., None])
    return (jnp.einsum('bhs,bhsd,bhse->bhde', w, k, v), jnp.einsum('bhs,bhsd->bhd', w, k), m)


def gla_scan(q, k, v, la, s0):
    tril = np.tril(np.ones((GLA_CHUNK, GLA_CHUNK), bool))[..., None]

    def step(s, inp):
        qc, kc, vc, lc = inp
        bc = jnp.cumsum(lc, axis=2)
        decay = jnp.exp(jnp.where(tril, bc[:, :, :, None, :] - bc[:, :, None, :, :], NEG))
        att = jnp.einsum('bhtd,bhsd,bhtsd->bhts', qc, kc, decay)
        o = jnp.einsum('bhts,bhse->bhte', att, vc) + jnp.einsum('bhtd,bhde->bhte', qc * jnp.exp(bc), s)
        b_end = bc[:, :, -1:, :]
        s_new = jnp.exp(b_end[:, :, 0, :, None]) * s + jnp.einsum('bhsd,bhse->bhde', kc * jnp.exp(b_end - bc), vc)
        return s_new, o

    s_fin, o = lax.scan(step, s0, (to_chunks(q, GLA_CHUNK), to_chunks(k, GLA_CHUNK), to_chunks(v, GLA_CHUNK),
                                   to_chunks(la, GLA_CHUNK)))
    return from_chunks(o), s_fin


def gla_final_state(_q, k, v, la):
    b = jnp.cumsum(la, axis=2)
    return jnp.einsum('bhsd,bhse->bhde', k * jnp.exp(b[:, :, -1:] - b), v)


def prefix_bidir(scan_fn, final_fn, lat_dirs, ctx_dirs, state0, need_ctx):
    h_lat, h_ctx = None, None
    for reverse, lat, ctx in zip((False, True), lat_dirs, ctx_dirs):
        flip = (lambda t: None if t is None else jnp.flip(t, 2)) if reverse else (lambda t: t)
        lat = [flip(t) for t in lat]
        ctx = [flip(t) for t in ctx]
        if need_ctx:
            hc, state = scan_fn(*ctx, state0)
            hc = flip(hc)
            h_ctx = hc if h_ctx is None else h_ctx + hc
        else:
            state = final_fn(*ctx)
        hl, _ = scan_fn(*lat, state)
        hl = flip(hl)
        h_lat = hl if h_lat is None else h_lat + hl
    return h_lat, h_ctx


def na_mixer(a, ac, w_kvq, rpb, w_o, need_ctx):
    B, N, D = a.shape
    L = ac.shape[1]
    H, d = NA_HEADS, NA_HD
    rows = N // GRID_W
    kr = min(NA_WIN_R, rows)
    k, v, q = jnp.split(a @ w_kvq, 3, axis=-1)
    kg = k.reshape(B, rows, GRID_W, H, d)
    vg = v.reshape(B, rows, GRID_W, H, d)
    qg = (q * d ** -0.5).reshape(B, rows, GRID_W, H, d)
    pc = ac @ (w_kvq if need_ctx else w_kvq[:, :2 * D])
    kc = pc[..., :D].reshape(B, L, H, d)
    vc = pc[..., D:2 * D].reshape(B, L, H, d)
    n_cb = GRID_W // NA_QBLK_C
    qcol = np.arange(GRID_W).reshape(n_cb, NA_QBLK_C)
    win0 = np.clip(qcol - NA_WIN_C // 2, 0, GRID_W - NA_WIN_C)
    kcol = np.clip(win0[:, :1], 0, GRID_W - NA_KBLK_C) + np.arange(NA_KBLK_C)
    in_win = (kcol[:, None, :] >= win0[:, :, None]) & (kcol[:, None, :] < win0[:, :, None] + NA_WIN_C)
    col_off = np.clip(kcol[:, None, :] - qcol[:, :, None], 1 - NA_WIN_C, NA_WIN_C - 1) + NA_WIN_C - 1
    rpb_f = rpb.astype(jnp.float32)
    n_loc = kr * NA_KBLK_C

    def one_row(r):
        r0 = jnp.clip(r - kr // 2, 0, rows - kr)
        k_blk = lax.dynamic_slice_in_dim(kg, r0, kr, axis=1)[:, :, kcol]
        v_blk = lax.dynamic_slice_in_dim(vg, r0, kr, axis=1)[:, :, kcol]
        q_row = lax.dynamic_index_in_dim(qg, r, axis=1, keepdims=False).reshape(B, n_cb, NA_QBLK_C, H, d)
        row_off = r0 + jnp.arange(kr) - r + NA_WIN_R - 1
        bias = rpb_f[:, row_off[None, None, :, None], col_off[:, :, None, :]]
        s_loc = jnp.einsum('bjqhd,brjkhd->bhjqrk', q_row, k_blk).astype(jnp.float32) + bias[None]
        s_loc = jnp.where(in_win[None, None, :, :, None, :], s_loc, NEG)
        s_ctx = jnp.einsum('bjqhd,blhd->bhjql', q_row, kc).astype(jnp.float32)
        s = jnp.concatenate([s_loc.reshape(B, H, n_cb, NA_QBLK_C, n_loc), s_ctx], axis=-1)
        p = jax.nn.softmax(s, axis=-1).astype(v.dtype)
        p_loc = p[..., :n_loc].reshape(B, H, n_cb, NA_QBLK_C, kr, NA_KBLK_C)
        o = jnp.einsum('bhjqrk,brjkhd->bjqhd', p_loc, v_blk) + jnp.einsum('bhjql,blhd->bjqhd', p[..., n_loc:], vc)
        return o.reshape(B, GRID_W, D)

    o = lax.map(one_row, jnp.arange(rows))
    y = jnp.transpose(o, (1, 0, 2, 3)).reshape(B, N, D) @ w_o
    yc = None
    if need_ctx:
        qc = (pc[..., 2 * D:] * d ** -0.5).reshape(B, L, H, d)
        yc = softmax_attend(qc, kc, vc).reshape(B, L, D) @ w_o
    return y, yc


def mlstm_mixer(a, ac, w_in, gate_b, conv_w, norm_g, w_o, need_ctx):
    B, N, D = a.shape
    H = ML_HEADS
    n_kvg = 2 * D + 4 * H

    def heads(t):
        return jnp.transpose(t.reshape(t.shape[0], t.shape[1], H, -1), (0, 2, 1, 3)).astype(jnp.float32)

    def prep(p, with_q):
        k = heads(jax.nn.silu(dwconv_centred(p[..., :D], conv_w[:, :D]))) * ML_HD ** -0.5
        v = heads(p[..., D:2 * D])
        g = jnp.transpose((p[..., 2 * D:n_kvg] + gate_b).astype(jnp.float32), (0, 2, 1))
        q = heads(jax.nn.silu(dwconv_centred(p[..., n_kvg:n_kvg + D], conv_w[:, D:]))) if with_q else None
        fwd = (q, k, v, g[:, :H], jax.nn.log_sigmoid(g[:, H:2 * H]))
        bwd = (q, k, v, g[:, 2 * H:3 * H], jax.nn.log_sigmoid(g[:, 3 * H:]))
        return fwd, bwd

    p = a @ w_in
    pc = ac @ (w_in if need_ctx else w_in[:, :n_kvg])
    zero = (jnp.zeros((B, H, ML_HD, ML_HD), jnp.float32), jnp.zeros((B, H, ML_HD), jnp.float32),
            jnp.zeros((B, H), jnp.float32))
    h, hc = prefix_bidir(mlstm_scan, mlstm_final_state, prep(p, True), prep(pc, need_ctx), zero, need_ctx)

    def readout(hh, pp):
        hh = jnp.transpose(hh, (0, 2, 1, 3)).astype(a.dtype)
        T = hh.shape[1]
        return (jax.nn.sigmoid(pp[..., n_kvg + D:]) * head_rmsnorm(hh, norm_g).reshape(B, T, D)) @ w_o

    return readout(h, p), (readout(hc, pc) if need_ctx else None)


def diff_mixer(a, ac, w_kvq, lam_p, norm_g, w_o, layer_idx, need_ctx):
    B, N, D = a.shape
    L = ac.shape[1]
    H, d = DA_HEADS, DA_HD
    lam_init = 0.8 - 0.6 * math.exp(-0.3 * layer_idx)
    lp = lam_p.astype(jnp.float32)
    lam = jnp.exp(jnp.sum(lp[0] * lp[1])) - jnp.exp(jnp.sum(lp[2] * lp[3])) + lam_init

    def attend(q, k, v):
        s = jnp.einsum('bthjd,bshjd->bhjts', q, k).astype(jnp.float32)
        p = jax.nn.softmax(s, axis=-1)
        w = (p[:, :, 0] - lam * p[:, :, 1]).astype(v.dtype)
        return jnp.einsum('bhts,bshe->bthe', w, v)

    def finish(o):
        T = o.shape[1]
        return (head_rmsnorm(o, norm_g) * (1.0 - lam_init)).reshape(B, T, D) @ w_o

    p = a @ w_kvq
    cos, sin = axial_rope_tables(N, d)
    cos, sin = cos[:, None, None, :], sin[:, None, None, :]
    k = apply_axial_rope(p[..., :D].reshape(B, N, H, 2, d), cos, sin)
    v = p[..., D:2 * D].reshape(B, N, H, 2 * d)
    q = apply_axial_rope((p[..., 2 * D:] * d ** -0.5).reshape(B, N, H, 2, d), cos, sin)
    pc = ac @ (w_kvq if need_ctx else w_kvq[:, :2 * D])
    kc = pc[..., :D].reshape(B, L, H, 2, d)
    vc = pc[..., D:2 * D].reshape(B, L, H, 2 * d)
    k_all = jnp.concatenate([k, kc], axis=1)
    v_all = jnp.concatenate([v, vc], axis=1)
    n_blk = N // DA_QBLK
    q_blocks = jnp.moveaxis(q.reshape(B, n_blk, DA_QBLK, H, 2, d), 1, 0)
    o = lax.map(lambda qb: attend(qb, k_all, v_all), q_blocks)
    y = finish(jnp.moveaxis(o, 0, 1).reshape(B, N, H, 2 * d))
    yc = None
    if need_ctx:
        qc = (pc[..., 2 * D:] * d ** -0.5).reshape(B, L, H, 2, d)
        yc = finish(attend(qc, kc, vc))
    return y, yc


def gla_mixer(a, ac, w_in, w_gate_up, b_gate, norm_g, w_o, need_ctx):
    B, N, D = a.shape
    H = GLA_HEADS
    dk_t = H * GLA_DK
    n_kvg = dk_t + D + 2 * GLA_RANK

    def heads(t):
        return jnp.transpose(t.reshape(t.shape[0], t.shape[1], H, -1), (0, 2, 1, 3)).astype(jnp.float32)

    def prep(p, with_q):
        k = heads(p[..., :dk_t])
        v = heads(p[..., dk_t:dk_t + D])
        low = p[..., dk_t + D:n_kvg]
        la = [heads(jax.nn.log_sigmoid((low[..., r * GLA_RANK:(r + 1) * GLA_RANK] @ w_gate_up[r] + b_gate[r])
                                       .astype(jnp.float32))) / GLA_TAU for r in range(2)]
        q = heads(p[..., n_kvg:n_kvg + dk_t]) * GLA_DK ** -0.5 if with_q else None
        return (q, k, v, la[0]), (q, k, v, la[1])

    p = a @ w_in
    pc = ac @ (w_in if need_ctx else w_in[:, :n_kvg])
    zero = jnp.zeros((B, H, GLA_DK, GLA_DV), jnp.float32)
    o, oc = prefix_bidir(gla_scan, gla_final_state, prep(p, True), prep(pc, need_ctx), zero, need_ctx)

    def readout(oo, pp):
        oo = jnp.transpose(oo, (0, 2, 1, 3)).astype(a.dtype)
        T = oo.shape[1]
        return (head_rmsnorm(oo, norm_g).reshape(B, T, D) * jax.nn.silu(pp[..., n_kvg + dk_t:])) @ w_o

    return readout(o, p), (readout(oc, pc) if need_ctx else None)


def setup_inputs(seed: int = 0) -> dict:
    key = jax.random.key(seed)
    ks = jax.random.split(key, 32)
    D = D_MODEL
    n_na, n_ml, n_da, n_gla = LAYERS_PER_MIXER
    dk_t = GLA_HEADS * GLA_DK

    def nrm(i, shape, std):
        return std * jax.random.normal(ks[i], shape, jnp.float32)

    def gain(i, shape):
        return 1.0 + nrm(i, shape, 0.02)

    ig_b = nrm(14, (n_ml, 2, 1, ML_HEADS), 0.1)
    fg_b = jax.random.uniform(ks[15], (n_ml, 2, 1, ML_HEADS), jnp.float32, 3.0, 6.0)
    ml_gate_b = jnp.concatenate([ig_b, fg_b], axis=2).reshape(n_ml, 4 * ML_HEADS)
    return {
        'x': nrm(0, (BATCH, SEQ, D), 1.0),
        'c': nrm(1, (BATCH, D), 1.0),
        'ctx': nrm(2, (BATCH, CTX_LEN, D), 1.0),
        'c_ctx': nrm(3, (D,), 1.0),
        'ada_w': nrm(4, (DEPTH, D, N_MOD * D), 0.5 * D ** -0.5),
        'ada_b': nrm(5, (DEPTH, N_MOD * D), 0.01),
        'norm_g': gain(6, (DEPTH, 3, D)),
        'ffn_w13': nrm(7, (DEPTH, 2, D, 2 * D_FF), D ** -0.5),
        'ffn_w2': nrm(8, (DEPTH, 2, D_FF, D), D_FF ** -0.5),
        'final_g': gain(9, (D,)),
        'na_w_kvq': nrm(10, (n_na, D, 3 * D), D ** -0.5),
        'na_rpb': nrm(11, (n_na, NA_HEADS, 2 * NA_WIN_R - 1, 2 * NA_WIN_C - 1), 0.02),
        'na_w_o': nrm(12, (n_na, D, D), D ** -0.5),
        'ml_w_in': nrm(13, (n_ml, D, 4 * D + 4 * ML_HEADS), D ** -0.5),
        'ml_gate_b': ml_gate_b,
        'ml_conv_w': nrm(16, (n_ml, ML_CONV, 2 * D), ML_CONV ** -0.5),
        'ml_norm_g': gain(17, (n_ml, D)),
        'ml_w_o': nrm(18, (n_ml, D, D), D ** -0.5),
        'da_w_kvq': nrm(19, (n_da, D, 3 * D), D ** -0.5),
        'da_lam': nrm(20, (n_da, 4, DA_HD), 0.1),
        'da_norm_g': gain(21, (n_da, D)),
        'da_w_o': nrm(22, (n_da, D, D), D ** -0.5),
        'gla_w_in': nrm(23, (n_gla, D, 2 * dk_t + 2 * D + 2 * GLA_RANK), D ** -0.5),
        'gla_w_gate_up': nrm(24, (n_gla, 2, GLA_RANK, dk_t), GLA_RANK ** -0.5),
        'gla_b_gate': nrm(25, (n_gla, 2, dk_t), 0.1),
        'gla_norm_g': gain(26, (n_gla, D)),
        'gla_w_o': nrm(27, (n_gla, D, D), D ** -0.5),
    }


def reference(x, c, ctx, c_ctx, ada_w, ada_b, norm_g, ffn_w13, ffn_w2, final_g,
              na_w_kvq, na_rpb, na_w_o,
              ml_w_in, ml_gate_b, ml_conv_w, ml_norm_g, ml_w_o,
              da_w_kvq, da_lam, da_norm_g, da_w_o,
              gla_w_in, gla_w_gate_up, gla_b_gate, gla_norm_g, gla_w_o):
    D = D_MODEL
    s_lat = jax.nn.silu(c)
    s_ctx = jax.nn.silu(c_ctx)
    h, hc = x, ctx
    for i in range(DEPTH):
        kind, j = i % N_MIXERS, i // N_MIXERS
        need_ctx = i < DEPTH - 1
        mod = jnp.split((s_lat @ ada_w[i] + ada_b[i])[:, None, :], N_MOD, axis=-1)
        n_c = N_MOD if need_ctx else 5
        modc = jnp.split((s_ctx @ ada_w[i, :, :n_c * D] + ada_b[i, :n_c * D])[None, None, :], n_c, axis=-1)
        h = h + ffn_branch(h, norm_g[i, 0], mod[0], mod[1], mod[2], ffn_w13[i, 0], ffn_w2[i, 0])
        hc = hc + ffn_branch(hc, norm_g[i, 0], modc[0], modc[1], modc[2], ffn_w13[i, 0], ffn_w2[i, 0])
        a = modulate(rmsnorm(h, norm_g[i, 1]), mod[3], mod[4])
        ac = modulate(rmsnorm(hc, norm_g[i, 1]), modc[3], modc[4])
        if kind == 0:
            y, yc = na_mixer(a, ac, na_w_kvq[j], na_rpb[j], na_w_o[j], need_ctx)
        elif kind == 1:
            y, yc = mlstm_mixer(a, ac, ml_w_in[j], ml_gate_b[j], ml_conv_w[j], ml_norm_g[j], ml_w_o[j], need_ctx)
        elif kind == 2:
            y, yc = diff_mixer(a, ac, da_w_kvq[j], da_lam[j], da_norm_g[j], da_w_o[j], i, need_ctx)
        else:
            y, yc = gla_mixer(a, ac, gla_w_in[j], gla_w_gate_up[j], gla_b_gate[j], gla_norm_g[j], gla_w_o[j], need_ctx)
        h = h + mod[5] * y
        h = h + ffn_branch(h, norm_g[i, 2], mod[6], mod[7], mod[8], ffn_w13[i, 1], ffn_w2[i, 1])
        if need_ctx:
            hc = hc + modc[5] * yc
            hc = hc + ffn_branch(hc, norm_g[i, 2], modc[6], modc[7], modc[8], ffn_w13[i, 1], ffn_w2[i, 1])
    return rmsnorm(h, final_g)
```

```python
import contextlib
import math
import numpy as np
import concourse.bass as bass
import concourse.mybir as mybir
from concourse.bass_utils import run_bass_kernel_spmd

F32 = mybir.dt.float32
BF16 = mybir.dt.bfloat16
AF = mybir.ActivationFunctionType
ALU = mybir.AluOpType
AX = mybir.AxisListType

D = 1024
T = 2304
NCTX = 256
NLAT = 2048
KC = 8
DFF = 2816
NEG = -1e30
EPS = 1e-6
GROUPS = [(0, 256, 1)] + [(256 + 512 * k, 512, 0) for k in range(4)]


class Slot:
    __slots__ = ("name", "last_w", "readers", "excl")

    def __init__(self, name, readers=(), excl=False):
        self.name = name
        self.last_w = None
        self.readers = list(readers)
        self.excl = excl


class Op:
    __slots__ = ("eng", "fn", "deps", "signal", "count", "semkey")

    def __init__(self, eng, fn, semkey):
        self.eng = eng
        self.fn = fn
        self.deps = set()
        self.signal = False
        self.count = 0
        self.semkey = semkey


class Prog:
    ENGS = ("pe", "act", "dve", "pool", "sp")

    def __init__(self, nc):
        self.nc = nc
        self.ops = {e: [] for e in self.ENGS}
        self.all = []
        self.dma_keys = []
        self.bar = []

    def slot(self, name):
        return Slot(name, self.bar)

    def barrier(self):
        self.bar = [self.ops[e][-1] for e in ("pe", "act", "dve", "pool") if self.ops[e]]

    def op(self, eng, fn, r=(), w=(), dma=None, nowaw=False):
        semkey = eng if dma is None else ("dma", dma)
        if dma is not None and semkey not in self.dma_keys:
            self.dma_keys.append(semkey)
        o = Op(eng, fn, semkey)
        for s in r:
            if s.last_w is not None:
                o.deps.add(s.last_w)
            if s.excl:
                for rd in s.readers:
                    if rd.semkey != semkey:
                        o.deps.add(rd)
        for s in w:
            if s.last_w is not None and not (nowaw and s.last_w.semkey == semkey):
                o.deps.add(s.last_w)
            for rd in s.readers:
                o.deps.add(rd)
        for s in r:
            s.readers = [x for x in s.readers if x.semkey != semkey] + [o]
        for s in w:
            s.last_w = o
            s.readers = []
        o.deps.discard(o)
        self.ops[eng].append(o)
        self.all.append(o)
        return o

    def emit(self, final_ops):
        nc = self.nc
        for o in self.all:
            keep = set()
            for d in o.deps:
                if d.semkey == "pe" and o.semkey == "pe":
                    continue
                keep.add(d)
                d.signal = True
            o.deps = keep
        for o in final_ops:
            o.signal = True
        for o in self.all:
            if isinstance(o.semkey, tuple):
                o.signal = True
        counts = {}
        for o in self.all:
            if o.signal:
                inc = 16 if isinstance(o.semkey, tuple) else 1
                counts[o.semkey] = counts.get(o.semkey, 0) + inc
                o.count = counts[o.semkey]
        self.counts = counts
        with contextlib.ExitStack() as es:
            sems = {}
            for k in list(self.ENGS) + self.dma_keys:
                nm = k if isinstance(k, str) else "d_" + str(k[1])
                sems[k] = es.enter_context(nc.semaphore("s_" + nm))
            block = es.enter_context(nc.Block())
            self.stats = {e: [0, 0] for e in self.ENGS}

            def run(engname, eng):
                seen = {}
                for o in self.ops[engname]:
                    need = {}
                    for d in o.deps:
                        if d.count > need.get(d.semkey, 0):
                            need[d.semkey] = d.count
                    for k, v in need.items():
                        if v > seen.get(k, 0):
                            eng.wait_ge(sems[k], v)
                            seen[k] = v
                            self.stats[engname][1] += 1
                    ins = o.fn(eng)
                    self.stats[engname][0] += 1
                    if o.signal:
                        ins.then_inc(sems[o.semkey], 16 if isinstance(o.semkey, tuple) else 1)
                if engname == "sp":
                    for o in final_ops:
                        eng.wait_ge(sems[o.semkey], o.count)

            @block.tensor
            def _(e):
                run("pe", e)

            @block.scalar
            def _(e):
                run("act", e)

            @block.vector
            def _(e):
                run("dve", e)

            @block.gpsimd
            def _(e):
                run("pool", e)

            @block.sync
            def _(e):
                run("sp", e)


def pipeline(items, stage_a, stage_b, look=2):
    n = len(items)
    for i in range(n + look):
        if i < n:
            stage_a(items[i])
        if i >= look:
            stage_b(items[i - look])


def pipeline_n(items, stages):
    n, S = len(items), len(stages)
    for t in range(n + S - 1):
        for k, st in enumerate(stages):
            i = t - k
            if 0 <= i < n:
                st(items[i])


class Rot:
    def __init__(self, items):
        self.items = items
        self.i = 0

    def next(self):
        it = self.items[self.i % len(self.items)]
        self.i += 1
        return it


class Kern:
    def __init__(self, stop=None, debug_out=False, layers=(0, 1, 2, 3)):
        self.layers = layers
        self.stop = stop
        self.debug_out = debug_out
        self.nc = bass.Bass("TRN2", target_bir_lowering=False)
        self.P = Prog(self.nc)
        self.dram = {}

    def MM(self, out, lhsT, rhs, start, stop, r, w):
        self.P.op("pe", lambda e: e.matmul(out, lhsT, rhs, start=start, stop=stop), r=r, w=w)

    def ACT(self, out, in_, func, r, w, bias=None, scale=None):
        kw = {}
        if bias is not None:
            kw["bias"] = bias
        if scale is not None:
            kw["scale"] = scale
        self.P.op("act", lambda e: e.activation(out=out, in_=in_, func=func, **kw), r=r, w=w)

    def STT(self, eng, out, in0, scalar, in1, op0, op1, r, w):
        self.P.op(eng, lambda e: e.scalar_tensor_tensor(out=out, in0=in0, scalar=scalar, in1=in1, op0=op0, op1=op1), r=r, w=w)

    def TS(self, eng, out, in0, s1, s2, op0, op1, r, w):
        if op1 is None:
            self.P.op(eng, lambda e: e.tensor_scalar(out=out, in0=in0, scalar1=s1, scalar2=None, op0=op0), r=r, w=w)
        else:
            self.P.op(eng, lambda e: e.tensor_scalar(out=out, in0=in0, scalar1=s1, scalar2=s2, op0=op0, op1=op1), r=r, w=w)

    def TT(self, eng, out, in0, in1, op, r, w):
        self.P.op(eng, lambda e: e.tensor_tensor(out=out, in0=in0, in1=in1, op=op), r=r, w=w)

    def CP(self, eng, out, in_, r, w):
        if eng == "act":
            self.P.op("act", lambda e: e.copy(out, in_), r=r, w=w)
        else:
            self.P.op(eng, lambda e: e.tensor_copy(out, in_), r=r, w=w)

    def RECIP(self, out, in_, r, w):
        self.P.op("dve", lambda e: e.reciprocal(out, in_), r=r, w=w)

    def MEMSET(self, eng, ap, val, w):
        self.P.op(eng, lambda e: e.memset(ap, val), w=w)

    def DMA(self, eng, out, in_, key, r=(), w=(), nowaw=False):
        return self.P.op(eng, lambda e: e.dma_start(out=out, in_=in_), r=r, w=w, dma=key, nowaw=nowaw)

    def din(self, name, shape):
        t = self.nc.dram_tensor(name, list(shape), F32, kind="ExternalInput").ap()
        self.dram[name] = t
        return t

    def carve(self, nbytes):
        off = self.ar_off
        self.ar_off += (nbytes + 63) // 64 * 64
        assert self.ar_off <= self.AR_BYTES, (self.ar_off, self.AR_BYTES)
        return off

    def abuf(self, shape, dt):
        esz = 4 if dt == F32 else 2
        n = int(np.prod(shape[1:]))
        off = self.carve(n * esz)
        v = self.arena[:, off // 4:(off + n * esz) // 4]
        if dt != F32:
            v = v.bitcast(dt)
        if len(shape) == 3:
            v = v.rearrange("p (a b) -> p a b", b=shape[2])
        elif len(shape) == 4:
            v = v.rearrange("p (a b c) -> p a b c", b=shape[2], c=shape[3])
        return v

    def phase(self):
        self.ar_off = 0
        self.P.barrier()

    def build(self):
        nc = self.nc
        d = self.din
        xT = d("xT", [D, T])
        cs = d("cs", [128, KC, 2])
        ada_w = d("ada_w", [4, D, 9 * D])
        adab = d("adab", [128, 4, 72])
        normg = d("normg", [128, 4, 3, KC])
        finalg = d("finalg", [128, KC])
        ffn_w13 = d("ffn_w13", [4, 2, D, 2 * DFF])
        ffn_w2 = d("ffn_w2", [4, 2, DFF, D])
        d("na_w_kvq", [D, 3 * D])
        d("na_w_o", [D, D])
        d("na_bias", [16, 20, 128, 512])
        d("da_w_kvq", [D, 3 * D])
        d("da_w_rot", [D, 2 * D])
        d("da_w_o", [D, D])
        d("da_lam", [128, 4, 64])
        d("da_norm_g", [128, KC])
        d("rope", [128, 2, NLAT])
        d("ml_w_in", [D, 4112])
        d("ml_w_o", [D, D])
        d("ml_gate_b", [128, 16])
        d("ml_conv", [128, 16, 3])
        d("ml_norm_g", [128, KC])
        d("gla_w_in", [D, 3104])
        d("gla_w_o", [D, D])
        d("gla_wgu", [33, 2, 512])
        d("gla_norm_g", [128, KC])
        d("cmask", [128, 2, 896])
        d("ident", [128, 128])
        if self.debug_out:
            outT = nc.dram_tensor("outT", [D, T], F32, kind="ExternalOutput").ap()
        else:
            outT = nc.dram_tensor("outT", [D, NLAT], F32, kind="ExternalOutput").ap()
        self.outT = outT
        P = self.P
        with contextlib.ExitStack() as es:
            def sb(name, shape, dt):
                return es.enter_context(nc.sbuf_tensor(name, shape, dt))

            self.hT = sb("hT", [128, KC, T], F32)
            self.UT = sb("UT", [128, KC, T], BF16)
            self.WS = [sb(f"WS{i}", [128, KC, 768], BF16) for i in range(2)]
            self.ones_bf = sb("ones_bf", [128, 128], BF16)
            self.ident = sb("ident_sb", [128, 128], F32)
            self.ident_bf = sb("ident_bf", [128, 128], BF16)
            self.cmask = sb("cmask_sb", [128, 2, 896], F32)
            self.sT = sb("sT", [128, KC, 2], BF16)
            self.csb = sb("csb", [128, KC, 2], F32)
            self.modTT = sb("modT", [128, 2, 72, 2], F32)
            self.AscT = sb("Asc", [128, 2, 3, KC, 2], F32)
            self.GscT = sb("Gsc", [128, 2, 3, KC, 2], F32)
            self.normg = sb("normg_sb", [128, 4, 3, KC], F32)
            self.adab = sb("adab_sb", [128, 4, 72], F32)
            self.finalg = sb("finalg_sb", [128, KC], F32)
            self.epsb = sb("epsb", [128, 1], F32)
            self.AR_BYTES = 64 * 1024
            self.arena = sb("arena", [128, self.AR_BYTES // 4], F32)
            self.ar_off = 0
            self.ps = [es.enter_context(nc.psum_tensor(f"ps{i}", [128, 512], F32)) for i in range(8)]
            self.PS = [Slot(f"ps{i}", excl=True) for i in range(8)]
            self.HS = [[Slot(f"h{c}_{g}") for g in range(5)] for c in range(KC)]
            self.US = [[Slot(f"u{c}_{g}") for g in range(5)] for c in range(KC)]
            self.WSs = [Slot(f"ws{i}") for i in range(2)]
            self.ws_i = 0
            self.S_const = Slot("const")
            self.S_mods = [Slot("mod0"), Slot("mod1")]
            self.S_scs = [Slot("sc0"), Slot("sc1")]

            init_slots = []
            last = None
            for c in range(KC):
                last = self.DMA("sp", self.hT[:, c, :], xT[c * 128:(c + 1) * 128, :], "init", w=self.HS[c])
                init_slots += self.HS[c]
            for dst, src in ((self.csb, cs), (self.normg, normg), (self.adab, adab), (self.finalg, finalg),
                             (self.ident, self.dram["ident"]), (self.cmask, self.dram["cmask"])):
                last = self.DMA("sp", dst[:], src, "init", w=[self.S_const])
            init_slots.append(self.S_const)
            for s in init_slots:
                s.last_w = last
            self.MEMSET("dve", self.ones_bf[:], 1.0, w=[self.S_const])
            self.CP("dve", self.ident_bf[:], self.ident[:], r=[self.S_const], w=[self.S_const])
            self.MEMSET("dve", self.epsb[:], EPS, w=[self.S_const])
            self.ACT(self.sT[:], self.csb[:], AF.Silu, r=[self.S_const], w=[self.S_const])

            for li, i in enumerate(self.layers):
                self.set_parity(i)
                if li == 0:
                    self.ada(i)
                self.ffn(i, 0, 0)
                if self.stop == ("ffn1", i):
                    break
                self.norm_mod(i, 1)
                [self.mix_na, self.mix_ml, self.mix_da, self.mix_gla][i](i)
                if self.stop == ("mix", i):
                    break
                self.ffn(i, 1, 2, ctx=(i < 3), ada_next=(i + 1 if li + 1 < len(self.layers) and self.stop is None else None))
                if self.stop == ("layer", i):
                    break
            fin = self.final()
            P.emit(fin)
        return nc

    def set_parity(self, i):
        p = i % 2
        self.modT, self.Asc, self.Gsc = self.modTT[:, p], self.AscT[:, p], self.GscT[:, p]
        self.S_mod, self.S_sc = self.S_mods[p], self.S_scs[p]

    def ada_piece(self, i, pc, WSb, WSs, key):
        src = self.dram["ada_w"][i].rearrange("(kc p) n -> p kc n", p=128)
        mp, mps = self.ps[7], self.PS[7]
        self.DMA("pool", WSb[:, :, 0:768], src[:, :, pc * 768:(pc + 1) * 768], key, w=[WSs], nowaw=True)
        for jc in range(6):
            J = pc * 6 + jc
            for kc in range(KC):
                self.MM(mp[:, 2 * J:2 * J + 2], WSb[:, kc, jc * 128:(jc + 1) * 128], self.sT[:, kc, :],
                        kc == 0, kc == KC - 1, r=[WSs, self.S_const], w=[mps])

    def ada_finish(self, i):
        p = i % 2
        modT, Asc, Gsc, S_mod, S_sc = self.modTT[:, p], self.AscT[:, p], self.GscT[:, p], self.S_mods[p], self.S_scs[p]
        mp, mps = self.ps[7], self.PS[7]
        mpv = mp[:, 0:144].rearrange("p (a b) -> p a b", b=2)
        self.TT("dve", modT, mpv, self.adab[:, i, :].unsqueeze(2).to_broadcast([128, 72, 2]), ALU.add,
                r=[mps, self.S_const], w=[S_mod])
        for sub in range(3):
            g = self.normg[:, i, sub, :].unsqueeze(2).to_broadcast([128, KC, 2])
            self.STT("dve", Asc[:, sub], modT[:, (3 * sub + 1) * 8:(3 * sub + 2) * 8, :], 1.0, g, ALU.add, ALU.mult,
                     r=[S_mod, self.S_const], w=[S_sc])
            self.TS("dve", Gsc[:, sub], modT[:, (3 * sub + 2) * 8:(3 * sub + 3) * 8, :], 0.5 if sub != 1 else 1.0, None,
                    ALU.mult, None, r=[S_mod], w=[S_sc])

    def ada(self, i):
        self.phase()
        for pc in range(12):
            s = self.ws_i % 2
            self.ws_i += 1
            self.ada_piece(i, pc, self.WS[s], self.WSs[s], f"ws{s}")
        self.ada_finish(i)

    def rstd_from_ps(self, ssp, sss, n, inv_n, tmp, tmps):
        self.ACT(tmp[:, :n], ssp[:, :n], AF.Ln, r=[sss, self.S_const], w=[tmps], bias=self.epsb[:, 0:1], scale=inv_n)
        self.ACT(tmp[:, :n], tmp[:, :n], AF.Exp, r=[tmps], w=[tmps], scale=-0.5)

    def norm_bufs(self, one_bank=False):
        P = self.P
        return dict(SQ=Rot([(self.abuf([128, 512], BF16), P.slot("sq")) for _ in range(2)]),
                    RS=Rot([(self.abuf([128, 512], F32), P.slot("rs")) for _ in range(2)]),
                    TM=Rot([(self.abuf([128, 512], F32), P.slot("tm")) for _ in range(3)]),
                    PSr=Rot([(self.ps[6], self.PS[6])] + ([] if one_bank else [(self.ps[7], self.PS[7])])))

    def norm_group(self, nb, sub, g):
        c0, n, lc = GROUPS[g]
        ssp, sss = nb["PSr"].next()
        for c in range(KC):
            sq, sqs = nb["SQ"].next()
            self.ACT(sq[:, :n], self.hT[:, c, c0:c0 + n], AF.Square, r=[self.HS[c][g]], w=[sqs])
            self.MM(ssp[:, :n], self.ones_bf[:], sq[:, :n], c == 0, c == KC - 1, r=[sqs, self.S_const], w=[sss])
        rs, rss = nb["RS"].next()
        self.rstd_from_ps(ssp, sss, n, 1.0 / D, rs, rss)
        for c in range(KC):
            tm, tms = nb["TM"].next()
            self.STT("dve", tm[:, :n], self.hT[:, c, c0:c0 + n], self.Asc[:, sub, c, lc:lc + 1], rs[:, :n], ALU.mult, ALU.mult,
                     r=[self.HS[c][g], self.S_sc, rss], w=[tms])
            self.ACT(self.UT[:, c, c0:c0 + n], tm[:, :n], AF.Identity, r=[tms, self.S_mod], w=[self.US[c][g]],
                     bias=self.modT[:, 3 * sub * 8 + c, lc:lc + 1], scale=1.0)

    def norm_mod(self, i, sub, ctx=True):
        self.phase()
        nb = self.norm_bufs()
        for g, (c0, n, lc) in enumerate(GROUPS):
            if lc and not ctx:
                continue
            self.norm_group(nb, sub, g)

    def ffn(self, i, f, sub, ctx=True, ada_next=None):
        self.phase()
        P = self.P
        w13 = self.dram["ffn_w13"][i, f].rearrange("(kc p) n -> p kc n", p=128)
        w2 = self.dram["ffn_w2"][i, f].rearrange("(j p) n -> p j n", p=128)
        W2 = [self.abuf([128, 3, D], BF16) for _ in range(2)]
        W2s = [P.slot(f"w2s{k}") for k in range(2)]
        SG = Rot([(self.abuf([128, 512], F32), P.slot("sg")) for _ in range(2)])
        AB = Rot([[(self.abuf([128, 512], BF16), P.slot("ab")) for _ in range(3)] for _ in range(2)])
        PG = Rot([(self.ps[0], self.PS[0]), (self.ps[1], self.PS[1])])
        PU = Rot([(self.ps[2], self.PS[2]), (self.ps[3], self.PS[3])])
        PY = Rot([(self.ps[4], self.PS[4]), (self.ps[5], self.PS[5])])
        pieces = [(0, 3), (3, 3), (6, 3), (9, 3), (12, 3), (15, 3), (18, 2), (20, 2)]
        nb = self.norm_bufs(one_bank=ada_next is not None)
        if ada_next is not None:
            ADW = self.abuf([128, KC, 768], BF16)
            ADWs = P.slot("adw")
        active = [g for g, (c0, n, lc) in enumerate(GROUPS) if not (lc and not ctx)]
        for pi, (j0, npr) in enumerate(pieces):
            s = self.ws_i % 2
            self.ws_i += 1
            WSb, WSs = self.WS[s], self.WSs[s]
            self.DMA("pool", WSb[:, :, 0:npr * 128], w13[:, :, j0 * 128:(j0 + npr) * 128], f"ws{s}", w=[WSs], nowaw=True)
            self.DMA("pool", WSb[:, :, 384:384 + npr * 128], w13[:, :, DFF + j0 * 128:DFF + (j0 + npr) * 128], f"ws{s}", w=[WSs], nowaw=True)
            self.DMA("pool", W2[s][:, 0:npr, :], w2[:, j0:j0 + npr, :], f"w2s{s}", w=[W2s[s]])
            for gi, g in enumerate(active):
                c0, n, lc = GROUPS[g]
                if pi == 0:
                    if gi == 0:
                        self.norm_group(nb, sub, g)
                    if gi + 1 < len(active):
                        self.norm_group(nb, sub, active[gi + 1])
                ab = AB.next()
                for jj in range(npr):
                    pg, pgs = PG.next()
                    pu, pus = PU.next()
                    for kc in range(KC):
                        self.MM(pg[:, :n], WSb[:, kc, jj * 128:(jj + 1) * 128], self.UT[:, kc, c0:c0 + n], kc == 0, kc == KC - 1,
                                r=[WSs, self.US[kc][g]], w=[pgs])
                    for kc in range(KC):
                        self.MM(pu[:, :n], WSb[:, kc, 384 + jj * 128:384 + (jj + 1) * 128], self.UT[:, kc, c0:c0 + n], kc == 0, kc == KC - 1,
                                r=[WSs, self.US[kc][g]], w=[pus])
                    sg, sgs = SG.next()
                    self.ACT(sg[:, :n], pg[:, :n], AF.Silu, r=[pgs], w=[sgs])
                    self.TT("dve", ab[jj][0][:, :n], sg[:, :n], pu[:, :n], ALU.mult, r=[sgs, pus], w=[ab[jj][1]])
                for c in range(KC):
                    py, pys = PY.next()
                    for jj in range(npr):
                        self.MM(py[:, :n], W2[s][:, jj, c * 128:(c + 1) * 128], ab[jj][0][:, :n], jj == 0, jj == npr - 1,
                                r=[W2s[s], ab[jj][1]], w=[pys])
                    self.STT("dve", self.hT[:, c, c0:c0 + n], py[:, :n], self.Gsc[:, sub, c, lc:lc + 1], self.hT[:, c, c0:c0 + n],
                             ALU.mult, ALU.add, r=[pys, self.S_sc, self.HS[c][g]], w=[self.HS[c][g]])
                if ada_next is not None and gi < 2:
                    pc = pi * 12 // 8 + gi
                    if pc < (pi + 1) * 12 // 8:
                        self.ada_piece(ada_next, pc, ADW, ADWs, "adw")
        if ada_next is not None:
            self.ada_finish(ada_next)

    def wo_partial(self, wo_name, chunks, RTs, RTslots, WOb, WOs, PY, ctx=True, key=None):
        wo = self.dram[wo_name].rearrange("(kc p) n -> p kc n", p=128)
        nk = len(chunks)
        for q, kc in enumerate(chunks):
            self.DMA("pool", WOb[:, q, :], wo[:, kc, :], key or f"wo{q}", w=[WOs[q]], nowaw=key is not None)
        for g, (c0, n, lc) in enumerate(GROUPS):
            if lc and not ctx:
                continue
            for c in range(KC):
                py, pys = PY.next()
                for q in range(nk):
                    self.MM(py[:, :n], WOb[:, q, c * 128:(c + 1) * 128], RTs[q][:, c0:c0 + n], q == 0, q == nk - 1,
                            r=[WOs[q], RTslots[q][g]], w=[pys])
                self.STT("dve", self.hT[:, c, c0:c0 + n], py[:, :n], self.Gsc[:, 1, c, lc:lc + 1], self.hT[:, c, c0:c0 + n],
                         ALU.mult, ALU.add, r=[pys, self.S_sc, self.HS[c][g]], w=[self.HS[c][g]])

    def mix_na(self, i):
        self.phase()
        P = self.P
        wkvq = self.dram["na_w_kvq"].rearrange("(kc p) n -> p kc n", p=128)
        nab = self.dram["na_bias"]
        KTz = [self.abuf([128, T], BF16) for _ in range(2)]
        QT = self.abuf([128, T], BF16)
        VA = [self.abuf([128, 18, 128], BF16) for _ in range(2)]
        OT = self.abuf([128, T], BF16)
        KTs = [P.slot("kt") for _ in range(5)]
        QTs = [P.slot("qt") for _ in range(5)]
        VAs = [P.slot("va0"), P.slot("va1")]
        OTs = [P.slot(f"ot{g}") for g in range(5)]
        BT = Rot([(self.abuf([128, 512], BF16), P.slot("bt"), k) for k in range(4)])
        PT = Rot([(self.abuf([128, 512], BF16), P.slot("pt")) for _ in range(5)])
        RD = Rot([(self.abuf([128, 512], F32), P.slot("rd")) for _ in range(2)])
        WOb = self.abuf([128, 1, D], BF16)
        WOs = [P.slot("wo")]
        PSS = Rot([(self.ps[k], self.PS[k]) for k in (0, 1, 2)])
        PO = Rot([(self.ps[3], self.PS[3]), (self.ps[4], self.PS[4])])
        PP = Rot([(self.ps[k], self.PS[k]) for k in (5, 6, 7)])
        PY = PP
        self.MEMSET("pool", VA[0][:, :, 64:128], 1.0, w=[VAs[0]])
        self.MEMSET("pool", VA[1][:, :, 0:64], 1.0, w=[VAs[1]])
        self.MEMSET("pool", KTz[0][64:128, :], 0.0, w=KTs)
        self.MEMSET("pool", KTz[1][0:64, :], 0.0, w=KTs)
        local_tiles = {0: [(2 + a // 2, k) for k, a in enumerate(range(0, 12, 2))],
                       1: [(2 + a // 2, 6 + k) for k, a in enumerate(range(4, 20, 2))],
                       2: [(2 + a // 2, 6 + k) for k, a in enumerate(range(12, 28, 2))],
                       3: [(2 + a // 2, 14 + k) for k, a in enumerate(range(20, 32, 2))]}
        for hp in range(8):
            s = self.ws_i % 2
            self.ws_i += 1
            WSb, WSs = self.WS[s], self.WSs[s]
            for k3 in range(3):
                self.DMA("pool", WSb[:, :, k3 * 128:(k3 + 1) * 128], wkvq[:, :, k3 * D + hp * 128:k3 * D + (hp + 1) * 128],
                         f"ws{s}", w=[WSs], nowaw=True)
            for g, (c0, n, lc) in enumerate(GROUPS):
                for (dsts, wc, sc) in ((KTs, 0, 1.0), (QTs, 2, 0.125)):
                    pp, pps = PP.next()
                    for kc in range(KC):
                        self.MM(pp[:, :n], WSb[:, kc, wc * 128:(wc + 1) * 128], self.UT[:, kc, c0:c0 + n], kc == 0, kc == KC - 1,
                                r=[WSs, self.US[kc][g]], w=[pps])
                    if wc == 0:
                        self.ACT(KTz[0][0:64, c0:c0 + n], pp[0:64, :n], AF.Identity, r=[pps], w=[dsts[g]], scale=sc)
                        self.ACT(KTz[1][64:128, c0:c0 + n], pp[64:128, :n], AF.Identity, r=[pps], w=[dsts[g]], scale=sc)
                    else:
                        self.ACT(QT[:, c0:c0 + n], pp[:, :n], AF.Identity, r=[pps], w=[dsts[g]], scale=sc)
            for t4 in range(0, 18, 4):
                nt = min(4, 18 - t4)
                pp, pps = PP.next()
                for q in range(nt):
                    tt = t4 + q
                    g = 0 if tt < 2 else 1 + (tt - 2) // 4
                    for kc in range(KC):
                        self.MM(pp[:, q * 128:(q + 1) * 128], self.UT[:, kc, tt * 128:(tt + 1) * 128], WSb[:, kc, 128:256], kc == 0, kc == KC - 1,
                                r=[WSs, self.US[kc][g]], w=[pps])
                ppv = pp[:, 0:nt * 128].rearrange("p (a b) -> p a b", b=128)
                self.CP("dve", VA[0][:, t4:t4 + nt, 0:64], ppv[:, :, 0:64], r=[pps], w=[VAs[0]])
                self.CP("act", VA[1][:, t4:t4 + nt, 64:128], ppv[:, :, 64:128], r=[pps], w=[VAs[1]])
            items = []
            for hh in range(2):
                for g, (c0, n, lc) in enumerate(GROUPS):
                    tiles = [(0, None), (1, None)]
                    if not lc:
                        tiles = local_tiles[g - 1] + tiles
                    for ti, (kt, bi) in enumerate(tiles):
                        items.append(dict(hh=hh, g=g, c0=c0, n=n, kt=kt, bi=bi, first=ti == 0, last=ti == len(tiles) - 1))
            cur = {}

            def st_a(it):
                hh, g, c0, n, kt, bi = it["hh"], it["g"], it["c0"], it["n"], it["kt"], it["bi"]
                hb = hh * 64
                sp_, sps = PSS.next()
                kg = 0 if kt < 2 else 1 + (kt - 2) // 4
                if bi is not None:
                    bt, bts, bk = BT.next()
                    self.DMA("pool", bt[:], nab[2 * hp + hh, bi], f"nb{bk}", w=[bts])
                self.MM(sp_[:, :n], KTz[hh][:, kt * 128:(kt + 1) * 128], QT[:, c0:c0 + n], True, bi is None,
                        r=[KTs[kg], QTs[g]], w=[sps])
                if bi is not None:
                    self.MM(sp_[:, :n], self.ident_bf[:], bt[:, :n], False, True, r=[bts, self.S_const], w=[sps])
                pt, pts = PT.next()
                self.ACT(pt[:, :n], sp_[:, :n], AF.Exp, r=[sps], w=[pts])
                it["pt"] = (pt, pts)

            def st_b(it):
                hh, g, c0, n, kt = it["hh"], it["g"], it["c0"], it["n"], it["kt"]
                hb = hh * 64
                db = 64 - hb
                if it["first"]:
                    cur["po"] = PO.next()
                po, pos = cur["po"]
                pt, pts = it["pt"]
                self.MM(po[:, :n], VA[hh][:, kt, :], pt[:, :n], it["first"], it["last"], r=[VAs[hh], pts], w=[pos])
                if it["last"]:
                    rd, rds = RD.next()
                    self.ACT(rd[db:db + 64, :n], po[db:db + 64, :n], AF.Ln, r=[pos], w=[rds])
                    self.ACT(rd[db:db + 64, :n], rd[db:db + 64, :n], AF.Exp, r=[rds], w=[rds], scale=-1.0)
                    self.CP("dve", rd[hb:hb + 64, :n], rd[db:db + 64, :n], r=[rds], w=[rds])
                    self.TT("dve", OT[hb:hb + 64, c0:c0 + n], po[hb:hb + 64, :n], rd[hb:hb + 64, :n], ALU.mult, r=[pos, rds], w=[OTs[g]])

            pipeline(items, st_a, st_b, look=2)
            self.wo_partial("na_w_o", [hp], [OT], [OTs], WOb, WOs, PY)

    def subphase(self, off):
        self.ar_off = off
        self.P.barrier()

    def mix_ml(self, i):
        self.phase()
        P = self.P
        win = self.dram["ml_w_in"].rearrange("(kc p) n -> p kc n", p=128)
        WG = self.abuf([128, KC, 16], BF16)
        gb = self.abuf([128, 16], F32)
        cv = self.abuf([128, 16, 3], F32)
        ng = self.abuf([128, KC], F32)
        GT = self.abuf([128, 18, 16], F32)
        IG = self.abuf([128, 18, 8], F32)
        LF = self.abuf([128, 18, 8], F32)
        KBS = self.abuf([128, 18, 8], F32)
        BTK = self.abuf([128, 18, 8], F32)
        onesf = self.abuf([128, 128], F32)
        onec = self.abuf([128, 1], F32)
        S_set = P.slot("mlset")
        S_g = P.slot("mlgates")
        KTh = [self.abuf([128, T], BF16) for _ in range(2)]
        QTh = [self.abuf([128, T], BF16) for _ in range(2)]
        VT = self.abuf([128, 18, 256], BF16)
        RTh = [self.abuf([128, T], BF16) for _ in range(2)]
        KTs, QTs, VTs = P.slot("kth"), P.slot("qth"), P.slot("vth")
        RTs = [[P.slot(f"rt{cc}_{g}") for g in range(5)] for cc in range(2)]
        off_sub = self.ar_off
        PSS = Rot([(self.ps[k], self.PS[k]) for k in (0, 1, 2)])
        NUM = [(self.ps[3], self.PS[3]), (self.ps[4], self.PS[4])]
        DEN = (self.ps[5], self.PS[5])
        PP = Rot([(self.ps[6], self.PS[6]), (self.ps[7], self.PS[7])])
        PY = PP
        cmf = self.cmask[:, 0, :]
        cmb = self.cmask[:, 1, :]
        self.DMA("pool", WG[:], win[:, :, 2048:2064], "misc", w=[S_set])
        self.DMA("sp", gb[:], self.dram["ml_gate_b"], "spm1", w=[S_set])
        self.DMA("sp", cv[:], self.dram["ml_conv"], "spm2", w=[S_set])
        self.DMA("sp", ng[:], self.dram["ml_norm_g"], "spm3", w=[S_set])
        self.MEMSET("dve", onesf[:], 1.0, w=[S_g])
        self.MEMSET("dve", onec[:], 1.0, w=[S_g])
        pp, pps = PP.next()
        for tt in range(18):
            g = 0 if tt < 2 else 1 + (tt - 2) // 4
            for kc in range(KC):
                self.MM(pp[:, tt * 16:(tt + 1) * 16], self.UT[:, kc, tt * 128:(tt + 1) * 128], WG[:, kc, :], kc == 0, kc == KC - 1,
                        r=[S_set, self.US[kc][g]], w=[pps])
        self.TT("dve", GT[:], pp[:, 0:288].rearrange("p (a b) -> p a b", b=16), gb[:, :].unsqueeze(1).to_broadcast([128, 18, 16]), ALU.add,
                r=[pps, S_set], w=[S_g])
        for d in range(2):
            self.CP("dve", IG[:, :, d * 4:(d + 1) * 4], GT[:, :, d * 8:d * 8 + 4], r=[S_g], w=[S_g])
            self.ACT(LF[:, :, d * 4:(d + 1) * 4], GT[:, :, d * 8 + 4:d * 8 + 8], AF.Exp, r=[S_g], w=[S_g], scale=-1.0)
        self.ACT(LF[:], LF[:], AF.Ln, r=[S_g], w=[S_g], bias=onec[:, 0:1], scale=1.0)
        self.TS("dve", LF[:], LF[:], -1.0, None, ALU.mult, None, r=[S_g], w=[S_g])
        pb, pbs = PP.next()
        for n_ in range(18):
            for d in range(2):
                if d == 0:
                    lst = [(i_, onesf[:]) for i_ in range(n_)] + [(n_, cmf[:, 384:512])]
                elif n_ < 2:
                    lst = [(n_, cmb[:, 384:512])] + [(i_, onesf[:]) for i_ in range(n_ + 1, 2)]
                else:
                    lst = [(0, onesf[:]), (1, onesf[:]), (n_, cmb[:, 384:512])] + [(i_, onesf[:]) for i_ in range(n_ + 1, 18)]
                for q, (i_, mk) in enumerate(lst):
                    self.MM(pb[:, n_ * 8 + d * 4:n_ * 8 + d * 4 + 4], mk, LF[:, i_, d * 4:(d + 1) * 4], q == 0, q == len(lst) - 1,
                            r=[S_g, self.S_const], w=[pbs])
        pbv = pb[:, 0:144].rearrange("p (a b) -> p a b", b=8)
        self.CP("act", BTK[:], pbv, r=[pbs], w=[S_g])
        self.STT("dve", KBS[:], IG[:], -4.0 * math.log(2.0), pbv, ALU.add, ALU.subtract, r=[S_g, pbs], w=[S_g])
        for h in range(4):
            self.subphase(off_sub)
            PK = self.abuf([128, T], F32)
            ACC = self.abuf([128, T], F32)
            PKs, ACCs = P.slot("pk"), P.slot("acc")
            s = self.ws_i % 2
            s2 = (self.ws_i + 1) % 2
            self.ws_i += 2
            WSb, WSs = self.WS[s], self.WSs[s]
            for k3, c_src in enumerate((h * 256, 1024 + h * 256, 2064 + h * 256)):
                self.DMA("pool", WSb[:, :, k3 * 256:(k3 + 1) * 256], win[:, :, c_src:c_src + 256], f"ws{s}", w=[WSs], nowaw=True)
            self.DMA("pool", self.WS[s2][:, :, 0:256], win[:, :, 3088 + h * 256:3088 + (h + 1) * 256], f"ws{s2}", w=[self.WSs[s2]], nowaw=True)
            for (dst, dsts, wc0, cvb) in ((KTh, KTs, 0, 0), (QTh, QTs, 512, 8)):
                for cc in range(2):
                    ch = cvb + 2 * h + cc
                    for g, (c0, n, lc) in enumerate(GROUPS):
                        pp, pps = PP.next()
                        for kc in range(KC):
                            self.MM(pp[:, :n], WSb[:, kc, wc0 + cc * 128:wc0 + (cc + 1) * 128], self.UT[:, kc, c0:c0 + n], kc == 0, kc == KC - 1,
                                    r=[WSs, self.US[kc][g]], w=[pps])
                        self.CP("act", PK[:, c0:c0 + n], pp[:, :n], r=[pps], w=[PKs])
                    self.TS("dve", ACC[:], PK[:], cv[:, ch, 1:2], None, ALU.mult, None, r=[PKs, S_set], w=[ACCs])
                    for (a, b) in ((0, NCTX), (NCTX, T)):
                        self.STT("dve", ACC[:, a + 1:b], PK[:, a:b - 1], cv[:, ch, 0:1], ACC[:, a + 1:b], ALU.mult, ALU.add,
                                 r=[PKs, S_set, ACCs], w=[ACCs])
                        self.STT("dve", ACC[:, a:b - 1], PK[:, a + 1:b], cv[:, ch, 2:3], ACC[:, a:b - 1], ALU.mult, ALU.add,
                                 r=[PKs, S_set, ACCs], w=[ACCs])
                    self.ACT(dst[cc][:], ACC[:], AF.Silu, r=[ACCs], w=[dsts])
            self.proj_v_tok(WSb, WSs, 256, 256, VT, VTs, PP)
            for cc in range(2):
                for g, (c0, n, lc) in enumerate(GROUPS):
                    pp, pps = PP.next()
                    for kc in range(KC):
                        self.MM(pp[:, :n], self.WS[s2][:, kc, cc * 128:(cc + 1) * 128], self.UT[:, kc, c0:c0 + n], kc == 0, kc == KC - 1,
                                r=[self.WSs[s2], self.US[kc][g]], w=[pps])
                    self.ACT(RTh[cc][:, c0:c0 + n], pp[:, :n], AF.Sigmoid, r=[pps], w=[RTs[cc][g]])
            self.subphase(off_sub)
            BREP = Rot([(self.abuf([128, 512], F32), P.slot("brep")) for _ in range(2)])
            DT = Rot([(self.abuf([128, 512], F32), P.slot("dt")) for _ in range(3)])
            AT = Rot([(self.abuf([128, 512], BF16), P.slot("at")) for _ in range(3)])
            AD = (self.abuf([128, 512], F32), P.slot("ad"))
            HS_ = [(self.abuf([128, 512], F32), P.slot(f"hsum{cc}")) for cc in range(2)]
            TMP = (self.abuf([128, 512], F32), P.slot("tmp"))
            SQ = Rot([(self.abuf([128, 512], BF16), P.slot("sq")) for _ in range(2)])
            items = []
            for g, (c0, n, lc) in enumerate(GROUPS):
                n0, m = c0 // 128, n // 128
                for d in range(2):
                    cm = cmf if d == 0 else cmb
                    diag = [(n0 + j, cm[:, 384 - 128 * j:384 - 128 * j + n]) for j in range(m)]
                    if d == 0:
                        tiles = [(i_, None) for i_ in range(n0)] + diag
                    elif lc:
                        tiles = diag
                    else:
                        tiles = [(0, None), (1, None)] + diag + [(i_, None) for i_ in range(n0 + m, 18)]
                    for ti, (it_, mk) in enumerate(tiles):
                        items.append(dict(g=g, c0=c0, n=n, n0=n0, m=m, d=d, it=it_, mk=mk, first=ti == 0, last=ti == len(tiles) - 1))
            cur = {}

            def st_a(it, h=h):
                g, c0, n, d, tl, mk = it["g"], it["c0"], it["n"], it["d"], it["it"], it["mk"]
                hd = d * 4 + h
                if it["first"]:
                    pb_, pbs_ = PP.next()
                    for q in range(it["m"]):
                        self.MM(pb_[:, q * 128:(q + 1) * 128], BTK[:, it["n0"] + q, hd:hd + 1].to_broadcast([128, 128]), self.ident[:], True, True,
                                r=[S_g, self.S_const], w=[pbs_])
                    cur["brep"] = BREP.next()
                    self.CP("act", cur["brep"][0][:, :n], pb_[:, :n], r=[pbs_], w=[cur["brep"][1]])
                brep, breps = cur["brep"]
                sp_, sps = PSS.next()
                for cc in range(2):
                    self.MM(sp_[:, :n], KTh[cc][:, tl * 128:(tl + 1) * 128], QTh[cc][:, c0:c0 + n], cc == 0, cc == 1, r=[KTs, QTs], w=[sps])
                dt, dts = DT.next()
                self.ACT(dt[:, :n], brep[:, :n], AF.Exp, r=[breps, S_g], w=[dts], bias=KBS[:, tl, hd:hd + 1], scale=1.0)
                if mk is not None:
                    self.TT("pool", dt[:, :n], dt[:, :n], mk, ALU.mult, r=[dts, self.S_const], w=[dts])
                at, ats = AT.next()
                self.TT("dve", at[:, :n], sp_[:, :n], dt[:, :n], ALU.mult, r=[sps, dts], w=[ats])
                it["at"] = (at, ats)

            def st_b(it, h=h):
                g, c0, n, d, tl = it["g"], it["c0"], it["n"], it["d"], it["it"]
                at, ats = it["at"]
                den, dens = DEN
                for cc in range(2):
                    self.MM(NUM[cc][0][:, :n], VT[:, tl, cc * 128:(cc + 1) * 128], at[:, :n], it["first"], it["last"], r=[VTs, ats], w=[NUM[cc][1]])
                self.MM(den[:, :n], self.ones_bf[:], at[:, :n], it["first"], it["last"], r=[self.S_const, ats], w=[dens])
                if not it["last"]:
                    return
                ad, ads = AD
                self.ACT(ad[:, :n], den[:, :n], AF.Abs, r=[dens], w=[ads])
                self.ACT(ad[:, :n], ad[:, :n], AF.Ln, r=[ads], w=[ads])
                self.TS("dve", ad[:, :n], ad[:, :n], 0.0, None, ALU.max, None, r=[ads], w=[ads])
                self.ACT(ad[:, :n], ad[:, :n], AF.Exp, r=[ads], w=[ads], scale=-1.0)
                for cc in range(2):
                    hs, hss = HS_[cc]
                    if d == 0:
                        self.TT("dve", hs[:, :n], NUM[cc][0][:, :n], ad[:, :n], ALU.mult, r=[NUM[cc][1], ads], w=[hss])
                    else:
                        tm, tms = TMP
                        self.TT("dve", tm[:, :n], NUM[cc][0][:, :n], ad[:, :n], ALU.mult, r=[NUM[cc][1], ads], w=[tms])
                        self.TT("pool", hs[:, :n], hs[:, :n], tm[:, :n], ALU.add, r=[hss, tms], w=[hss])
                if d == 0:
                    return
                pp, pps = PP.next()
                for cc in range(2):
                    sq, sqs = SQ.next()
                    self.ACT(sq[:, :n], HS_[cc][0][:, :n], AF.Square, r=[HS_[cc][1]], w=[sqs])
                    self.MM(pp[:, :n], self.ones_bf[:], sq[:, :n], cc == 0, cc == 1, r=[sqs, self.S_const], w=[pps])
                self.rstd_from_ps(pp, pps, n, 1.0 / 256, ad, ads)
                for cc in range(2):
                    tm, tms = TMP
                    self.STT("dve", tm[:, :n], HS_[cc][0][:, :n], ng[:, 2 * h + cc:2 * h + cc + 1], ad[:, :n], ALU.mult, ALU.mult,
                             r=[HS_[cc][1], S_set, ads], w=[tms])
                    self.TT("pool", RTh[cc][:, c0:c0 + n], RTh[cc][:, c0:c0 + n], tm[:, :n], ALU.mult, r=[RTs[cc][g], tms], w=[RTs[cc][g]])

            pipeline(items, st_a, st_b, look=2)
            WOb = self.WS[s2][:, 0:3, :].rearrange("p a b -> p (a b)")[:, 0:2 * D].rearrange("p (a b) -> p a b", b=D)
            WOs = [self.WSs[s2], self.WSs[s2]]
            self.wo_partial("ml_w_o", [2 * h, 2 * h + 1], RTh, RTs, WOb, WOs, PY, key=f"ws{s2}")

    def proj_v_tok(self, WSb, WSs, wc0, ncols, VT, VTs, PP):
        per = 512 // ncols
        for t4 in range(0, 18, per):
            nt = min(per, 18 - t4)
            pp, pps = PP.next()
            for q in range(nt):
                tt = t4 + q
                g = 0 if tt < 2 else 1 + (tt - 2) // 4
                for kc in range(KC):
                    self.MM(pp[:, q * ncols:(q + 1) * ncols], self.UT[:, kc, tt * 128:(tt + 1) * 128], WSb[:, kc, wc0:wc0 + ncols],
                            kc == 0, kc == KC - 1, r=[WSs, self.US[kc][g]], w=[pps])
            ppv = pp[:, 0:nt * ncols].rearrange("p (a b) -> p a b", b=ncols)
            self.CP("act", VT[:, t4:t4 + nt, :], ppv, r=[pps], w=[VTs])

    def mix_da(self, i):
        self.phase()
        P = self.P
        lam_init = 0.8 - 0.6 * math.exp(-0.3 * i)
        wkvq = self.dram["da_w_kvq"].rearrange("(kc p) n -> p kc n", p=128)
        wrot = self.dram["da_w_rot"].rearrange("(kc p) n -> p kc n", p=128)
        rope = self.dram["rope"]
        KTz = [self.abuf([128, T], BF16) for _ in range(2)]
        QT = self.abuf([128, T], BF16)
        VT = self.abuf([128, 18, 128], BF16)
        RT = self.abuf([128, T], BF16)
        KTs = [P.slot("kt") for _ in range(5)]
        QTs = [P.slot("qt") for _ in range(5)]
        VTs = P.slot("vt")
        RTs = [P.slot(f"rt{g}") for g in range(5)]
        CSb = Rot([(self.abuf([128, 2, 512], F32), P.slot("cs"), k) for k in range(2)])
        T1 = Rot([(self.abuf([128, 512], F32), P.slot("t1")) for _ in range(2)])
        T2 = Rot([(self.abuf([128, 512], F32), P.slot("t2")) for _ in range(2)])
        PT = Rot([(self.abuf([128, 512], BF16), P.slot("pt")) for _ in range(5)])
        R0 = (self.abuf([128, 512], F32), P.slot("r0"))
        R1 = (self.abuf([128, 512], F32), P.slot("r1"))
        OB = (self.abuf([128, 512], F32), P.slot("ob"))
        SQ = (self.abuf([128, 512], BF16), P.slot("sq"))
        RS = (self.abuf([128, 512], F32), P.slot("rs"))
        lamb = self.abuf([128, 4, 64], F32)
        lamt = self.abuf([128, 2, 64], F32)
        lams = self.abuf([128, 4], F32)
        ngb = self.abuf([128, KC], F32)
        S_l = P.slot("lam")
        WOb = self.abuf([128, 1, D], BF16)
        WOs = [P.slot("wo")]
        PSS = Rot([(self.ps[k], self.PS[k]) for k in (0, 1, 2)])
        ACC = Rot([((self.ps[3], self.PS[3]), (self.ps[4], self.PS[4])), ((self.ps[5], self.PS[5]), (self.ps[6], self.PS[6]))])
        PN = (self.ps[7], self.PS[7])
        PP = Rot([(self.ps[k], self.PS[k]) for k in (7, 0, 1, 2)])
        PY = PP
        self.MEMSET("pool", KTz[0][64:128, :], 0.0, w=KTs)
        self.MEMSET("pool", KTz[1][0:64, :], 0.0, w=KTs)
        self.DMA("sp", lamb[:], self.dram["da_lam"], "spm1", w=[S_l])
        self.DMA("sp", ngb[:], self.dram["da_norm_g"], "spm2", w=[S_l])
        for q in range(2):
            self.TT("dve", lamt[:, q, :], lamb[:, 2 * q, :], lamb[:, 2 * q + 1, :], ALU.mult, r=[S_l], w=[S_l])
            self.P.op("dve", lambda e, q=q: e.reduce_sum(lams[:, q:q + 1], lamt[:, q, :], axis=AX.X), r=[S_l], w=[S_l])
        self.ACT(lams[:, 0:2], lams[:, 0:2], AF.Exp, r=[S_l], w=[S_l])
        self.STT("dve", lams[:, 2:3], lams[:, 1:2], -lam_init, lams[:, 0:1], ALU.add, ALU.subtract, r=[S_l], w=[S_l])
        self.TS("dve", ngb[:], ngb[:], 1.0 - lam_init, None, ALU.mult, None, r=[S_l], w=[S_l])
        neglam = lams[:, 2:3]
        for h in range(8):
            s = self.ws_i % 2
            self.ws_i += 1
            WSb, WSs = self.WS[s], self.WSs[s]
            for k3 in range(3):
                self.DMA("pool", WSb[:, :, k3 * 128:(k3 + 1) * 128], wkvq[:, :, k3 * D + h * 128:k3 * D + (h + 1) * 128],
                         f"ws{s}", w=[WSs], nowaw=True)
            for k2 in range(2):
                self.DMA("pool", WSb[:, :, 384 + k2 * 128:384 + (k2 + 1) * 128], wrot[:, :, k2 * D + h * 128:k2 * D + (h + 1) * 128],
                         f"ws{s}", w=[WSs], nowaw=True)
            for g, (c0, n, lc) in enumerate(GROUPS):
                if not lc:
                    cs_, css, ck = CSb.next()
                    self.DMA("sp", cs_[:], rope[:, :, c0 - NCTX:c0 - NCTX + n], f"cs{ck}", w=[css])
                for (dst, dsts, wc, rc, sc) in ((None, KTs, 0, 384, 1.0), (QT, QTs, 256, 512, 0.125)):
                    pp, pps = PP.next()
                    for kc in range(KC):
                        self.MM(pp[:, :n], WSb[:, kc, wc:wc + 128], self.UT[:, kc, c0:c0 + n], kc == 0, kc == KC - 1,
                                r=[WSs, self.US[kc][g]], w=[pps])
                    if lc:
                        if dst is None:
                            self.ACT(KTz[0][0:64, c0:c0 + n], pp[0:64, :n], AF.Identity, r=[pps], w=[dsts[g]], scale=sc)
                            self.ACT(KTz[1][64:128, c0:c0 + n], pp[64:128, :n], AF.Identity, r=[pps], w=[dsts[g]], scale=sc)
                        else:
                            self.ACT(dst[:, c0:c0 + n], pp[:, :n], AF.Identity, r=[pps], w=[dsts[g]], scale=sc)
                        continue
                    pr, prs = PP.next()
                    for kc in range(KC):
                        self.MM(pr[:, :n], WSb[:, kc, rc:rc + 128], self.UT[:, kc, c0:c0 + n], kc == 0, kc == KC - 1,
                                r=[WSs, self.US[kc][g]], w=[prs])
                    t1, t1s = T1.next()
                    t2, t2s = T2.next()
                    self.STT("dve", t1[:, :n], pp[:, :n], sc, cs_[:, 0, :n], ALU.mult, ALU.mult, r=[pps, css], w=[t1s])
                    self.STT("dve", t2[:, :n], pr[:, :n], sc, cs_[:, 1, :n], ALU.mult, ALU.mult, r=[prs, css], w=[t2s])
                    if dst is None:
                        self.TT("pool", KTz[0][0:64, c0:c0 + n], t1[0:64, :n], t2[0:64, :n], ALU.add, r=[t1s, t2s], w=[dsts[g]])
                        self.TT("pool", KTz[1][64:128, c0:c0 + n], t1[64:128, :n], t2[64:128, :n], ALU.add, r=[t1s, t2s], w=[dsts[g]])
                    else:
                        self.TT("pool", dst[:, c0:c0 + n], t1[:, :n], t2[:, :n], ALU.add, r=[t1s, t2s], w=[dsts[g]])
            self.proj_v_tok(WSb, WSs, 128, 128, VT, VTs, PP)
            items = []
            for g, (c0, n, lc) in enumerate(GROUPS):
                tiles = [0, 1] if lc else list(range(18))
                for j in range(2):
                    for ti, kt in enumerate(tiles):
                        items.append(dict(g=g, c0=c0, n=n, j=j, kt=kt, first=ti == 0, last=ti == len(tiles) - 1))
            cur = {}

            def st_a(it):
                g, c0, n, j, kt = it["g"], it["c0"], it["n"], it["j"], it["kt"]
                jb = j * 64
                sp_, sps = PSS.next()
                kg = 0 if kt < 2 else 1 + (kt - 2) // 4
                self.MM(sp_[:, :n], KTz[j][:, kt * 128:(kt + 1) * 128], QT[:, c0:c0 + n], True, True,
                        r=[KTs[kg], QTs[g]], w=[sps])
                pt, pts = PT.next()
                self.ACT(pt[:, :n], sp_[:, :n], AF.Exp, r=[sps], w=[pts])
                it["pt"] = (pt, pts)

            def st_b(it, h=h):
                g, c0, n, j, kt = it["g"], it["c0"], it["n"], it["j"], it["kt"]
                if it["first"]:
                    cur["acc"] = ACC.next()
                (num, nums), (den, dens) = cur["acc"]
                pt, pts = it["pt"]
                self.MM(num[:, :n], VT[:, kt, :], pt[:, :n], it["first"], it["last"], r=[VTs, pts], w=[nums])
                self.MM(den[:, :n], self.ones_bf[:], pt[:, :n], it["first"], it["last"], r=[self.S_const, pts], w=[dens])
                if not it["last"]:
                    return
                rb, rbs = (R0, R1)[j]
                self.ACT(rb[:, :n], den[:, :n], AF.Ln, r=[dens], w=[rbs])
                self.ACT(rb[:, :n], rb[:, :n], AF.Exp, r=[rbs], w=[rbs], scale=-1.0)
                self.TT("dve", rb[:, :n], num[:, :n], rb[:, :n], ALU.mult, r=[nums, rbs], w=[rbs])
                if j == 0:
                    return
                ob, obs = OB
                self.STT("dve", ob[:, :n], R1[0][:, :n], neglam, R0[0][:, :n], ALU.mult, ALU.add, r=[R0[1], R1[1], S_l], w=[obs])
                sq, sqs = SQ
                self.ACT(sq[:, :n], ob[:, :n], AF.Square, r=[obs], w=[sqs])
                pp, pps = PN
                self.MM(pp[:, :n], self.ones_bf[:], sq[:, :n], True, True, r=[sqs, self.S_const], w=[pps])
                rs, rss = RS
                self.rstd_from_ps(pp, pps, n, 1.0 / 128, rs, rss)
                self.STT("dve", RT[:, c0:c0 + n], ob[:, :n], ngb[:, h:h + 1], rs[:, :n], ALU.mult, ALU.mult, r=[obs, rss, S_l], w=[RTs[g]])

            pipeline(items, st_a, st_b, look=2)
            self.wo_partial("da_w_o", [h], [RT], [RTs], WOb, WOs, PY)

    def mix_gla(self, i):
        self.phase()
        P = self.P
        win = self.dram["gla_w_in"].rearrange("(kc p) n -> p kc n", p=128)
        wo = self.dram["gla_w_o"].rearrange("(kc p) n -> p kc n", p=128)
        WL = self.abuf([128, KC, 32], BF16)
        wgu = self.abuf([128, 2, 512], F32)
        gng = self.abuf([128, KC], F32)
        onec = self.abuf([128, 1], F32)
        S_set = P.slot("glaset")
        kT = self.abuf([128, T], BF16)
        qT = self.abuf([128, T], BF16)
        ktok = self.abuf([128, 18, 128], BF16)
        VT = self.abuf([128, 18, 256], BF16)
        OH = self.abuf([128, 2, NLAT], F32)
        S32 = [self.abuf([128, 256], F32) for _ in range(2)]
        Sbf = [self.abuf([128, 256], BF16) for _ in range(2)]
        kTs, qTs, ktoks, VTs = P.slot("kT"), P.slot("qT"), P.slot("ktok"), P.slot("VT")
        OHs = [P.slot(f"oh{t}") for t in range(16)]
        S32s = [P.slot("s32f"), P.slot("s32b")]
        Sbfs = [P.slot("sbff"), P.slot("sbfb")]
        off_sub = self.ar_off
        PA = Rot([(self.ps[k], self.PS[k]) for k in range(4)])
        PO = Rot([(self.ps[4], self.PS[4]), (self.ps[5], self.PS[5])])
        PP = Rot([(self.ps[6], self.PS[6]), (self.ps[7], self.PS[7])])
        self.DMA("pool", WL[:], win[:, :, 1536:1568], "misc", w=[S_set])
        self.DMA("sp", wgu[0:33], self.dram["gla_wgu"], "spm1", w=[S_set])
        self.DMA("sp", gng[:], self.dram["gla_norm_g"], "spm2", w=[S_set])
        self.MEMSET("dve", onec[:], 1.0, w=[S_set])
        masks = [self.cmask[:, 0, 384:512], self.cmask[:, 1, 384:512]]
        orders = [list(range(18)), [1, 0] + list(range(17, 1, -1))]
        qscale = 128.0 ** -0.5
        for h in range(4):
            self.subphase(off_sub)
            L33 = Rot([(self.abuf([128, 128], F32), P.slot("l33")) for _ in range(4)])
            LA = Rot([(self.abuf([128, 128], F32), P.slot("la")) for _ in range(4)])
            EK = Rot([(self.abuf([128, 3, 128], F32), P.slot("ek")) for _ in range(3)])
            KQ = Rot([(self.abuf([128, 4, 128], BF16), P.slot("kq")) for _ in range(6)])
            SC = Rot([(self.abuf([128, 1], F32), P.slot("sc")) for _ in range(6)])
            UB = Rot([(self.abuf([128, 256], F32), P.slot("ub")) for _ in range(2)])
            s = self.ws_i % 2
            s2 = (self.ws_i + 1) % 2
            self.ws_i += 2
            WSb, WSs = self.WS[s], self.WSs[s]
            for (dc, sc_, wd_) in ((0, h * 128, 128), (128, 512 + h * 256, 256), (384, 1568 + h * 128, 128), (512, 2080 + h * 256, 256)):
                self.DMA("pool", WSb[:, :, dc:dc + wd_], win[:, :, sc_:sc_ + wd_], f"ws{s}", w=[WSs], nowaw=True)
            WOv = self.WS[s2][:, 0:3, :].rearrange("p a b -> p (a b)")[:, 0:2 * D].rearrange("p (a b) -> p a b", b=D)
            for q in range(2):
                self.DMA("pool", WOv[:, q, :], wo[:, 2 * h + q, :], f"ws{s2}", w=[self.WSs[s2]], nowaw=True)
            for it_ in L33.items:
                self.MEMSET("dve", it_[0][32:33, :], 1.0, w=[it_[1]])
            for d in range(2):
                self.MEMSET("dve", S32[d][:], 0.0, w=[S32s[d]])
                self.MEMSET("pool", Sbf[d][:], 0.0, w=[Sbfs[d]])
            for g, (c0, n, lc) in enumerate(GROUPS):
                for (dst, dsts, wc, sc) in ((kT, kTs, 0, 1.0), (qT, qTs, 384, qscale)):
                    pp, pps = PP.next()
                    for kc in range(KC):
                        self.MM(pp[:, :n], WSb[:, kc, wc:wc + 128], self.UT[:, kc, c0:c0 + n], kc == 0, kc == KC - 1,
                                r=[WSs, self.US[kc][g]], w=[pps])
                    self.ACT(dst[:, c0:c0 + n], pp[:, :n], AF.Identity, r=[pps], w=[dsts], scale=sc)
            self.proj_v_tok(WSb, WSs, 0, 128, ktok, ktoks, PP)
            self.proj_v_tok(WSb, WSs, 128, 256, VT, VTs, PP)
            touched = set()
            items = [dict(d=d, tt=orders[d][step]) for step in range(18) for d in range(2)]

            def st_a1(it, h=h):
                d, tt = it["d"], it["tt"]
                cols = slice(tt * 128, (tt + 1) * 128)
                gt = 0 if tt < 2 else 1 + (tt - 2) // 4
                pl, pls = PA.next()
                for kc in range(KC):
                    self.MM(pl[0:32, 0:128], WL[:, kc, :], self.UT[:, kc, cols], kc == 0, kc == KC - 1, r=[S_set, self.US[kc][gt]], w=[pls])
                l33, l33s = L33.next()
                self.CP("act", l33[0:32, :], pl[0:32, 0:128], r=[pls], w=[l33s])
                it["l33"] = (l33, l33s)

            def st_a2(it, h=h):
                d = it["d"]
                l33, l33s = it["l33"]
                pla, plas = PA.next()
                self.MM(pla[:, 0:128], l33[0:33, :], wgu[0:33, d, h * 128:(h + 1) * 128], True, True, r=[l33s, S_set], w=[plas])
                la, las = LA.next()
                self.ACT(la[:], pla[:, 0:128], AF.Exp, r=[plas], w=[las], scale=-1.0)
                self.ACT(la[:], la[:], AF.Ln, r=[las, S_set], w=[las], bias=onec[:, 0:1], scale=1.0)
                it["la"] = (la, las)

            def st_a3(it, h=h):
                d, tt = it["d"], it["tt"]
                cols = slice(tt * 128, (tt + 1) * 128)
                M = masks[d]
                la, las = it["la"]
                pbc, pbcs = PA.next()
                self.MM(pbc[:, 0:128], M, la[:], True, True, r=[las, self.S_const], w=[pbcs])
                self.MM(pbc[:, 128:256], la[:], M, True, True, r=[las, self.S_const], w=[pbcs])
                ek, eks = EK.next()
                self.ACT(ek[:, 0, :], pbc[:, 0:128], AF.Exp, r=[pbcs], w=[eks], scale=1.0 / 16)
                self.ACT(ek[:, 1, :], pbc[:, 128:256], AF.Exp, r=[pbcs], w=[eks], scale=1.0 / 16)
                self.ACT(ek[:, 2, :], pbc[:, 128:256], AF.Exp, r=[pbcs], w=[eks], scale=-1.0 / 16)
                kq, kqs = KQ.next()
                self.TT("dve", kq[:, 0, :], ktok[:, tt, :], ek[:, 0, :], ALU.mult, r=[ktoks, eks], w=[kqs])
                self.TT("pool", kq[:, 1, :], kT[:, cols], ek[:, 1, :], ALU.mult, r=[kTs, eks], w=[kqs])
                self.TT("dve", kq[:, 2, :], qT[:, cols], ek[:, 2, :], ALU.mult, r=[qTs, eks], w=[kqs])
                sc, scs = SC.next()
                te = 127 if d == 0 else 0
                self.CP("act", sc[:, 0:1], ek[:, 2, te:te + 1], r=[eks], w=[scs])
                it["sc"], it["kq"] = (sc, scs), (kq, kqs)

            def st_a4(it, h=h):
                d, tt = it["d"], it["tt"]
                if tt < 2:
                    return
                kq, kqs = it["kq"]
                pa, pas = PA.next()
                self.MM(pa[:, 0:128], kq[:, 1, :], kq[:, 2, :], True, True, r=[kqs], w=[pas])
                self.TT("dve", kq[:, 3, :], pa[:, 0:128], masks[d], ALU.mult, r=[pas, self.S_const, kqs], w=[kqs])

            def st_b(it, h=h):
                d, tt = it["d"], it["tt"]
                (sc, scs), (kq, kqs) = it["sc"], it["kq"]
                if tt >= 2:
                    po, pos = PO.next()
                    for ec in range(2):
                        self.MM(po[:, ec * 128:(ec + 1) * 128], VT[:, tt, ec * 128:(ec + 1) * 128], kq[:, 3, :], True, False,
                                r=[VTs, kqs], w=[pos])
                        self.MM(po[:, ec * 128:(ec + 1) * 128], Sbf[d][:, ec * 128:(ec + 1) * 128], kq[:, 2, :], False, True,
                                r=[Sbfs[d], kqs], w=[pos])
                    lt = tt - 2
                    ohv = OH[:, :, lt * 128:(lt + 1) * 128]
                    pov = po[:, 0:256].rearrange("p (a b) -> p a b", b=128)
                    if lt not in touched:
                        touched.add(lt)
                        self.CP("act", ohv, pov, r=[pos], w=[OHs[lt]])
                    else:
                        self.TT("dve", ohv, ohv, pov, ALU.add, r=[pos, OHs[lt]], w=[OHs[lt]])
                pst, psts = PP.next()
                self.MM(pst[:, 0:256], kq[:, 0, :], VT[:, tt, :], True, True, r=[kqs, VTs], w=[psts])
                ub, ubs = UB.next()
                self.TT("dve", ub[:], pst[:, 0:256], S32[d][:], ALU.add, r=[psts, S32s[d]], w=[ubs])
                self.ACT(S32[d][:], ub[:], AF.Identity, r=[ubs, scs], w=[S32s[d]], scale=sc[:, 0:1])
                self.TS("dve", Sbf[d][:], ub[:], sc[:, 0:1], None, ALU.mult, None, r=[ubs, scs], w=[Sbfs[d]])

            pipeline_n(items, [st_a1, st_a2, st_a3, st_a4, st_b])
            self.subphase(off_sub)
            SQ = Rot([(self.abuf([128, 512], BF16), P.slot("sq")) for _ in range(2)])
            RS = (self.abuf([128, 512], F32), P.slot("rs"))
            SR = Rot([(self.abuf([128, 512], F32), P.slot("sr")) for _ in range(2)])
            TM = Rot([(self.abuf([128, 512], F32), P.slot("tm")) for _ in range(2)])
            RTb = Rot([[(self.abuf([128, 512], BF16), P.slot("rt")) for _ in range(2)] for _ in range(2)])
            for g, (c0, n, lc) in enumerate(GROUPS):
                if lc:
                    continue
                l0 = c0 - NCTX
                ohs = OHs[l0 // 128:l0 // 128 + 4]
                pp, pps = PP.next()
                for cc in range(2):
                    sq, sqs = SQ.next()
                    self.ACT(sq[:, :n], OH[:, cc, l0:l0 + n], AF.Square, r=ohs, w=[sqs])
                    self.MM(pp[:, :n], self.ones_bf[:], sq[:, :n], cc == 0, cc == 1, r=[sqs, self.S_const], w=[pps])
                rs, rss = RS
                self.rstd_from_ps(pp, pps, n, 1.0 / 256, rs, rss)
                rt = RTb.next()
                for cc in range(2):
                    pr, prs = PP.next()
                    for kc in range(KC):
                        self.MM(pr[:, :n], WSb[:, kc, 512 + cc * 128:512 + (cc + 1) * 128], self.UT[:, kc, c0:c0 + n], kc == 0, kc == KC - 1,
                                r=[WSs, self.US[kc][g]], w=[prs])
                    sr, srs = SR.next()
                    self.ACT(sr[:, :n], pr[:, :n], AF.Silu, r=[prs], w=[srs])
                    tm, tms = TM.next()
                    self.STT("dve", tm[:, :n], OH[:, cc, l0:l0 + n], gng[:, 2 * h + cc:2 * h + cc + 1], rs[:, :n], ALU.mult, ALU.mult,
                             r=ohs + [S_set, rss], w=[tms])
                    self.TT("dve", rt[cc][0][:, :n], tm[:, :n], sr[:, :n], ALU.mult, r=[tms, srs], w=[rt[cc][1]])
                for c in range(KC):
                    py, pys = PO.next()
                    for cc in range(2):
                        self.MM(py[:, :n], WOv[:, cc, c * 128:(c + 1) * 128], rt[cc][0][:, :n], cc == 0, cc == 1,
                                r=[self.WSs[s2], rt[cc][1]], w=[pys])
                    self.STT("dve", self.hT[:, c, c0:c0 + n], py[:, :n], self.Gsc[:, 1, c, lc:lc + 1], self.hT[:, c, c0:c0 + n],
                             ALU.mult, ALU.add, r=[pys, self.S_sc, self.HS[c][g]], w=[self.HS[c][g]])

    def final(self):
        self.phase()
        P = self.P
        fin = []
        if self.debug_out:
            for c in range(KC):
                fin.append(self.DMA("sp", self.outT[c * 128:(c + 1) * 128, :], self.hT[:, c, :], f"out{c}", r=self.HS[c]))
            return fin
        SQ = Rot([(self.abuf([128, 512], BF16), P.slot("sq")) for _ in range(2)])
        RS = Rot([(self.abuf([128, 512], F32), P.slot("rs")) for _ in range(2)])
        OB = Rot([(self.abuf([128, 512], F32), P.slot("ob"), k) for k in range(4)])
        PSr = Rot([(self.ps[6], self.PS[6]), (self.ps[7], self.PS[7])])
        for g, (c0, n, lc) in enumerate(GROUPS):
            if lc:
                continue
            ssp, sss = PSr.next()
            for c in range(KC):
                sq, sqs = SQ.next()
                self.ACT(sq[:, :n], self.hT[:, c, c0:c0 + n], AF.Square, r=[self.HS[c][g]], w=[sqs])
                self.MM(ssp[:, :n], self.ones_bf[:], sq[:, :n], c == 0, c == KC - 1, r=[sqs, self.S_const], w=[sss])
            rs, rss = RS.next()
            self.rstd_from_ps(ssp, sss, n, 1.0 / D, rs, rss)
            for c in range(KC):
                ob, obs, k = OB.next()
                self.STT("dve", ob[:, :n], self.hT[:, c, c0:c0 + n], self.finalg[:, c:c + 1], rs[:, :n], ALU.mult, ALU.mult,
                         r=[self.HS[c][g], self.S_const, rss], w=[obs])
                fin.append(self.DMA("sp", self.outT[c * 128:(c + 1) * 128, c0 - NCTX:c0 - NCTX + n], ob[:, :n], f"out{k}", r=[obs]))
        return fin


def _fm(v):
    v = np.asarray(v, np.float32)
    lead = v.shape[:-1]
    return np.ascontiguousarray(np.moveaxis(v.reshape(lead + (KC, 128)), -1, 0))


def _na_bias(rpb):
    rpb = np.asarray(rpb, np.float32)
    specs = [(0, a) for a in range(0, 12, 2)] + [(1, a) for a in range(4, 20, 2)] + [(3, a) for a in range(20, 32, 2)]
    p = np.arange(128)
    x = np.arange(512)
    out = np.empty((16, 20, 128, 512), np.float32)
    for ti, (j, a) in enumerate(specs):
        kr = (a + p // 64)[:, None]
        kc = (p % 64)[:, None]
        qr = (8 * j + x // 64)[None, :]
        qc = (x % 64)[None, :]
        r0 = np.clip(qr - 4, 0, 24)
        w0 = np.clip(qc - 8, 0, 48)
        valid = (kr >= r0) & (kr < r0 + 8) & (kc >= w0) & (kc < w0 + 16)
        ro = np.clip(kr - qr + 7, 0, 14)
        co = np.clip(kc - qc, -15, 15) + 15
        out[:, ti] = np.where(valid[None], rpb[:, ro, co], np.float32(NEG))
    return out


def _consts():
    p = np.arange(128)[:, None]
    x = np.arange(896)[None, :]
    mf = (x - 384 >= p).astype(np.float32)
    mb = (x - 384 <= p).astype(np.float32)
    return np.ascontiguousarray(np.stack([mf, mb], axis=1)), np.eye(128, dtype=np.float32)


def _rope_tables():
    t = np.arange(NLAT)
    row = (t // 64).astype(np.float32)
    col = (t % 64).astype(np.float32)
    freqs = (np.float32(10000.0) ** (-np.arange(0, 32, 2, dtype=np.float32) / np.float32(32))).astype(np.float32)
    ar = row[:, None] * freqs
    ac = col[:, None] * freqs
    ang = np.concatenate([ar, ar, ac, ac], axis=-1)
    cos = np.cos(ang).astype(np.float32)
    sin = np.sin(ang).astype(np.float32)
    sgn = np.concatenate([-np.ones(16), np.ones(16), -np.ones(16), np.ones(16)]).astype(np.float32)
    tab = np.stack([cos.T, (sin * sgn[None, :]).T], axis=1)
    return np.ascontiguousarray(np.concatenate([tab, tab], axis=0).astype(np.float32))


def _prep(inputs, b):
    f32 = lambda a: np.ascontiguousarray(np.asarray(a, np.float32))
    m = {}
    m["xT"] = np.ascontiguousarray(np.concatenate([inputs["ctx"][b], inputs["x"][b]], axis=0).T.astype(np.float32))
    m["cs"] = np.ascontiguousarray(np.stack([_fm(inputs["c"][b]), _fm(inputs["c_ctx"])], axis=-1))
    return m


def _shared(inputs):
    f32 = lambda a: np.ascontiguousarray(np.asarray(a, np.float32))
    m = {}
    m["ada_w"] = f32(inputs["ada_w"])
    m["adab"] = np.ascontiguousarray(np.moveaxis(np.asarray(inputs["ada_b"], np.float32).reshape(4, 72, 128), -1, 0))
    m["normg"] = _fm(inputs["norm_g"])
    m["finalg"] = _fm(inputs["final_g"])
    m["ffn_w13"] = f32(inputs["ffn_w13"])
    m["ffn_w2"] = f32(inputs["ffn_w2"])
    m["na_w_kvq"] = f32(inputs["na_w_kvq"][0])
    m["na_w_o"] = f32(inputs["na_w_o"][0])
    m["na_bias"] = _na_bias(inputs["na_rpb"][0])
    m["cmask"], m["ident"] = _consts()
    wd = np.asarray(inputs["da_w_kvq"][0], np.float32)
    perm = np.concatenate([np.arange(16, 32), np.arange(0, 16), np.arange(48, 64), np.arange(32, 48)])
    pfull = (np.arange(0, D, 64)[:, None] + perm[None, :]).reshape(-1)
    m["da_w_kvq"] = f32(wd)
    m["da_w_rot"] = np.ascontiguousarray(np.concatenate([wd[:, 0:D][:, pfull], wd[:, 2 * D:3 * D][:, pfull]], axis=1))
    m["da_w_o"] = f32(inputs["da_w_o"][0])
    m["da_lam"] = np.ascontiguousarray(np.broadcast_to(np.asarray(inputs["da_lam"][0], np.float32)[None], (128, 4, 64)))
    m["da_norm_g"] = _fm(inputs["da_norm_g"][0])
    m["rope"] = _rope_tables()
    m["ml_w_in"] = f32(inputs["ml_w_in"][0])
    m["ml_w_o"] = f32(inputs["ml_w_o"][0])
    m["ml_gate_b"] = np.ascontiguousarray(np.broadcast_to(np.asarray(inputs["ml_gate_b"][0], np.float32)[None], (128, 16)))
    m["ml_conv"] = np.ascontiguousarray(np.transpose(np.asarray(inputs["ml_conv_w"][0], np.float32).reshape(3, 16, 128), (2, 1, 0)))
    m["ml_norm_g"] = _fm(inputs["ml_norm_g"][0])
    m["gla_w_in"] = f32(inputs["gla_w_in"][0])
    m["gla_w_o"] = f32(inputs["gla_w_o"][0])
    gu = np.asarray(inputs["gla_w_gate_up"][0], np.float32)
    gb = np.asarray(inputs["gla_b_gate"][0], np.float32)
    wgu = np.zeros((33, 2, 512), np.float32)
    wgu[0:16, 0] = gu[0]
    wgu[16:32, 1] = gu[1]
    wgu[32] = gb
    m["gla_wgu"] = wgu
    m["gla_norm_g"] = _fm(inputs["gla_norm_g"][0])
    return m


def run(inputs, stop=None, debug_out=False, ncores=8):
    k = Kern(stop=stop, debug_out=debug_out)
    nc = k.build()
    shared = _shared(inputs)
    in_maps = []
    for b in range(ncores):
        m = dict(shared)
        m.update(_prep(inputs, b))
        in_maps.append({n: m[n] for n in k.dram})
    res = run_bass_kernel_spmd(nc, in_maps, core_ids=list(range(ncores)))
    return [r["outT"] for r in res.results], k


def kernel(**inputs):
    outs, _ = run(inputs)
    return np.stack([np.ascontiguousarray(o.T) for o in outs], axis=0).astype(np.float32)
```

```python
import contextlib
import math
import numpy as np
import concourse.bass as bass
import concourse.mybir as mybir
from concourse.bass_utils import run_bass_kernel_spmd

F32 = mybir.dt.float32
BF16 = mybir.dt.bfloat16
AF = mybir.ActivationFunctionType
ALU = mybir.AluOpType
AX = mybir.AxisListType

D = 1024
T = 2304
NCTX = 256
NLAT = 2048
KC = 8
DFF = 2816
NEG = -1e30
EPS = 1e-6
GROUPS = [(0, 256, 1)] + [(256 + 512 * k, 512, 0) for k in range(4)]


class Slot:
    __slots__ = ("name", "last_w", "readers", "excl")

    def __init__(self, name, readers=(), excl=False):
        self.name = name
        self.last_w = None
        self.readers = list(readers)
        self.excl = excl


class Op:
    __slots__ = ("eng", "fn", "deps", "signal", "count", "semkey")

    def __init__(self, eng, fn, semkey):
        self.eng = eng
        self.fn = fn
        self.deps = set()
        self.signal = False
        self.count = 0
        self.semkey = semkey


class Prog:
    ENGS = ("pe", "act", "dve", "pool", "sp")

    def __init__(self, nc):
        self.nc = nc
        self.ops = {e: [] for e in self.ENGS}
        self.all = []
        self.dma_keys = []
        self.bar = []

    def slot(self, name):
        return Slot(name, self.bar)

    def barrier(self):
        self.bar = [self.ops[e][-1] for e in ("pe", "act", "dve", "pool") if self.ops[e]]

    def op(self, eng, fn, r=(), w=(), dma=None, nowaw=False):
        semkey = eng if dma is None else ("dma", dma)
        if dma is not None and semkey not in self.dma_keys:
            self.dma_keys.append(semkey)
        o = Op(eng, fn, semkey)
        for s in r:
            if s.last_w is not None:
                o.deps.add(s.last_w)
            if s.excl:
                for rd in s.readers:
                    if rd.semkey != semkey:
                        o.deps.add(rd)
        for s in w:
            if s.last_w is not None and not (nowaw and s.last_w.semkey == semkey):
                o.deps.add(s.last_w)
            for rd in s.readers:
                o.deps.add(rd)
        for s in r:
            s.readers = [x for x in s.readers if x.semkey != semkey] + [o]
        for s in w:
            s.last_w = o
            s.readers = []
        o.deps.discard(o)
        self.ops[eng].append(o)
        self.all.append(o)
        return o

    def emit(self, final_ops):
        nc = self.nc
        for o in self.all:
            keep = set()
            for d in o.deps:
                if d.semkey == "pe" and o.semkey == "pe":
                    continue
                keep.add(d)
                d.signal = True
            o.deps = keep
        for o in final_ops:
            o.signal = True
        for o in self.all:
            if isinstance(o.semkey, tuple):
                o.signal = True
        counts = {}
        for o in self.all:
            if o.signal:
                inc = 16 if isinstance(o.semkey, tuple) else 1
                counts[o.semkey] = counts.get(o.semkey, 0) + inc
                o.count = counts[o.semkey]
        self.counts = counts
        with contextlib.ExitStack() as es:
            sems = {}
            for k in list(self.ENGS) + self.dma_keys:
                nm = k if isinstance(k, str) else "d_" + str(k[1])
                sems[k] = es.enter_context(nc.semaphore("s_" + nm))
            block = es.enter_context(nc.Block())
            self.stats = {e: [0, 0] for e in self.ENGS}

            def run(engname, eng):
                seen = {}
                for o in self.ops[engname]:
                    need = {}
                    for d in o.deps:
                        if d.count > need.get(d.semkey, 0):
                            need[d.semkey] = d.count
                    for k, v in need.items():
                        if v > seen.get(k, 0):
                            eng.wait_ge(sems[k], v)
                            seen[k] = v
                            self.stats[engname][1] += 1
                    ins = o.fn(eng)
                    self.stats[engname][0] += 1
                    if o.signal:
                        ins.then_inc(sems[o.semkey], 16 if isinstance(o.semkey, tuple) else 1)
                if engname == "sp":
                    for o in final_ops:
                        eng.wait_ge(sems[o.semkey], o.count)

            @block.tensor
            def _(e):
                run("pe", e)

            @block.scalar
            def _(e):
                run("act", e)

            @block.vector
            def _(e):
                run("dve", e)

            @block.gpsimd
            def _(e):
                run("pool", e)

            @block.sync
            def _(e):
                run("sp", e)


def pipeline(items, stage_a, stage_b, look=2):
    n = len(items)
    for i in range(n + look):
        if i < n:
            stage_a(items[i])
        if i >= look:
            stage_b(items[i - look])


def pipeline_n(items, stages):
    n, S = len(items), len(stages)
    for t in range(n + S - 1):
        for k, st in enumerate(stages):
            i = t - k
            if 0 <= i < n:
                st(items[i])


class Rot:
    def __init__(self, items):
        self.items = items
        self.i = 0

    def next(self):
        it = self.items[self.i % len(self.items)]
        self.i += 1
        return it


class Kern:
    def __init__(self, stop=None, debug_out=False, layers=(0, 1, 2, 3)):
        self.layers = layers
        self.stop = stop
        self.debug_out = debug_out
        self.nc = bass.Bass("TRN2", target_bir_lowering=False)
        self.P = Prog(self.nc)
        self.dram = {}

    def MM(self, out, lhsT, rhs, start, stop, r, w):
        self.P.op("pe", lambda e: e.matmul(out, lhsT, rhs, start=start, stop=stop), r=r, w=w)

    def ACT(self, out, in_, func, r, w, bias=None, scale=None):
        kw = {}
        if bias is not None:
            kw["bias"] = bias
        if scale is not None:
            kw["scale"] = scale
        self.P.op("act", lambda e: e.activation(out=out, in_=in_, func=func, **kw), r=r, w=w)

    def STT(self, eng, out, in0, scalar, in1, op0, op1, r, w):
        self.P.op(eng, lambda e: e.scalar_tensor_tensor(out=out, in0=in0, scalar=scalar, in1=in1, op0=op0, op1=op1), r=r, w=w)

    def TS(self, eng, out, in0, s1, s2, op0, op1, r, w):
        if op1 is None:
            self.P.op(eng, lambda e: e.tensor_scalar(out=out, in0=in0, scalar1=s1, scalar2=None, op0=op0), r=r, w=w)
        else:
            self.P.op(eng, lambda e: e.tensor_scalar(out=out, in0=in0, scalar1=s1, scalar2=s2, op0=op0, op1=op1), r=r, w=w)

    def TT(self, eng, out, in0, in1, op, r, w):
        self.P.op(eng, lambda e: e.tensor_tensor(out=out, in0=in0, in1=in1, op=op), r=r, w=w)

    def CP(self, eng, out, in_, r, w):
        if eng == "act":
            self.P.op("act", lambda e: e.copy(out, in_), r=r, w=w)
        else:
            self.P.op(eng, lambda e: e.tensor_copy(out, in_), r=r, w=w)

    def RECIP(self, out, in_, r, w):
        self.P.op("dve", lambda e: e.reciprocal(out, in_), r=r, w=w)

    def MEMSET(self, eng, ap, val, w):
        self.P.op(eng, lambda e: e.memset(ap, val), w=w)

    def DMA(self, eng, out, in_, key, r=(), w=(), nowaw=False):
        return self.P.op(eng, lambda e: e.dma_start(out=out, in_=in_), r=r, w=w, dma=key, nowaw=nowaw)

    def din(self, name, shape):
        t = self.nc.dram_tensor(name, list(shape), F32, kind="ExternalInput").ap()
        self.dram[name] = t
        return t

    def carve(self, nbytes):
        off = self.ar_off
        self.ar_off += (nbytes + 63) // 64 * 64
        assert self.ar_off <= self.AR_BYTES, (self.ar_off, self.AR_BYTES)
        return off

    def abuf(self, shape, dt):
        esz = 4 if dt == F32 else 2
        n = int(np.prod(shape[1:]))
        off = self.carve(n * esz)
        v = self.arena[:, off // 4:(off + n * esz) // 4]
        if dt != F32:
            v = v.bitcast(dt)
        if len(shape) == 3:
            v = v.rearrange("p (a b) -> p a b", b=shape[2])
        elif len(shape) == 4:
            v = v.rearrange("p (a b c) -> p a b c", b=shape[2], c=shape[3])
        return v

    def phase(self):
        self.ar_off = 0
        self.P.barrier()

    def build(self):
        nc = self.nc
        d = self.din
        xT = d("xT", [D, T])
        cs = d("cs", [128, KC, 2])
        ada_w = d("ada_w", [4, D, 9 * D])
        adab = d("adab", [128, 4, 72])
        normg = d("normg", [128, 4, 3, KC])
        finalg = d("finalg", [128, KC])
        ffn_w13 = d("ffn_w13", [4, 2, D, 2 * DFF])
        ffn_w2 = d("ffn_w2", [4, 2, DFF, D])
        d("na_w_kvq", [D, 3 * D])
        d("na_w_o", [D, D])
        d("na_bias", [16, 20, 128, 512])
        d("da_w_kvq", [D, 3 * D])
        d("da_w_rot", [D, 2 * D])
        d("da_w_o", [D, D])
        d("da_lam", [128, 4, 64])
        d("da_norm_g", [128, KC])
        d("rope", [128, 2, NLAT])
        d("ml_w_in", [D, 4112])
        d("ml_w_o", [D, D])
        d("ml_gate_b", [128, 16])
        d("ml_conv", [128, 16, 3])
        d("ml_norm_g", [128, KC])
        d("gla_w_in", [D, 3104])
        d("gla_w_o", [D, D])
        d("gla_wgu", [33, 2, 512])
        d("gla_norm_g", [128, KC])
        d("cmask", [128, 2, 896])
        d("ident", [128, 128])
        if self.debug_out:
            outT = nc.dram_tensor("outT", [D, T], F32, kind="ExternalOutput").ap()
        else:
            outT = nc.dram_tensor("outT", [D, NLAT], F32, kind="ExternalOutput").ap()
        self.outT = outT
        P = self.P
        with contextlib.ExitStack() as es:
            def sb(name, shape, dt):
                return es.enter_context(nc.sbuf_tensor(name, shape, dt))

            self.hT = sb("hT", [128, KC, T], F32)
            self.UT = sb("UT", [128, KC, T], BF16)
            self.WS = [sb(f"WS{i}", [128, KC, 768], BF16) for i in range(2)]
            self.ones_bf = sb("ones_bf", [128, 128], BF16)
            self.ident = sb("ident_sb", [128, 128], F32)
            self.ident_bf = sb("ident_bf", [128, 128], BF16)
            self.cmask = sb("cmask_sb", [128, 2, 896], F32)
            self.sT = sb("sT", [128, KC, 2], BF16)
            self.csb = sb("csb", [128, KC, 2], F32)
            self.modTT = sb("modT", [128, 2, 72, 2], F32)
            self.AscT = sb("Asc", [128, 2, 3, KC, 2], F32)
            self.GscT = sb("Gsc", [128, 2, 3, KC, 2], F32)
            self.normg = sb("normg_sb", [128, 4, 3, KC], F32)
            self.adab = sb("adab_sb", [128, 4, 72], F32)
            self.finalg = sb("finalg_sb", [128, KC], F32)
            self.epsb = sb("epsb", [128, 1], F32)
            self.AR_BYTES = 64 * 1024
            self.arena = sb("arena", [128, self.AR_BYTES // 4], F32)
            self.ar_off = 0
            self.ps = [es.enter_context(nc.psum_tensor(f"ps{i}", [128, 512], F32)) for i in range(8)]
            self.PS = [Slot(f"ps{i}", excl=True) for i in range(8)]
            self.HS = [[Slot(f"h{c}_{g}") for g in range(5)] for c in range(KC)]
            self.US = [[Slot(f"u{c}_{g}") for g in range(5)] for c in range(KC)]
            self.WSs = [Slot(f"ws{i}") for i in range(2)]
            self.ws_i = 0
            self.S_const = Slot("const")
            self.S_mods = [Slot("mod0"), Slot("mod1")]
            self.S_scs = [Slot("sc0"), Slot("sc1")]

            init_slots = []
            last = None
            for c in range(KC):
                last = self.DMA("sp", self.hT[:, c, :], xT[c * 128:(c + 1) * 128, :], "init", w=self.HS[c])
                init_slots += self.HS[c]
            for dst, src in ((self.csb, cs), (self.normg, normg), (self.adab, adab), (self.finalg, finalg),
                             (self.ident, self.dram["ident"]), (self.cmask, self.dram["cmask"])):
                last = self.DMA("sp", dst[:], src, "init", w=[self.S_const])
            init_slots.append(self.S_const)
            for s in init_slots:
                s.last_w = last
            self.MEMSET("dve", self.ones_bf[:], 1.0, w=[self.S_const])
            self.CP("dve", self.ident_bf[:], self.ident[:], r=[self.S_const], w=[self.S_const])
            self.MEMSET("dve", self.epsb[:], EPS, w=[self.S_const])
            self.ACT(self.sT[:], self.csb[:], AF.Silu, r=[self.S_const], w=[self.S_const])

            for li, i in enumerate(self.layers):
                self.set_parity(i)
                if li == 0:
                    self.ada(i)
                self.ffn(i, 0, 0)
                if self.stop == ("ffn1", i):
                    break
                self.norm_mod(i, 1)
                [self.mix_na, self.mix_ml, self.mix_da, self.mix_gla][i](i)
                if self.stop == ("mix", i):
                    break
                self.ffn(i, 1, 2, ctx=(i < 3), ada_next=(i + 1 if li + 1 < len(self.layers) and self.stop is None else None))
                if self.stop == ("layer", i):
                    break
            fin = self.final()
            P.emit(fin)
        return nc

    def set_parity(self, i):
        p = i % 2
        self.modT, self.Asc, self.Gsc = self.modTT[:, p], self.AscT[:, p], self.GscT[:, p]
        self.S_mod, self.S_sc = self.S_mods[p], self.S_scs[p]

    def ada_piece(self, i, pc, WSb, WSs, key):
        src = self.dram["ada_w"][i].rearrange("(kc p) n -> p kc n", p=128)
        mp, mps = self.ps[7], self.PS[7]
        self.DMA("pool", WSb[:, :, 0:768], src[:, :, pc * 768:(pc + 1) * 768], key, w=[WSs], nowaw=True)
        for jc in range(6):
            J = pc * 6 + jc
            for kc in range(KC):
                self.MM(mp[:, 2 * J:2 * J + 2], WSb[:, kc, jc * 128:(jc + 1) * 128], self.sT[:, kc, :],
                        kc == 0, kc == KC - 1, r=[WSs, self.S_const], w=[mps])

    def ada_finish(self, i):
        p = i % 2
        modT, Asc, Gsc, S_mod, S_sc = self.modTT[:, p], self.AscT[:, p], self.GscT[:, p], self.S_mods[p], self.S_scs[p]
        mp, mps = self.ps[7], self.PS[7]
        mpv = mp[:, 0:144].rearrange("p (a b) -> p a b", b=2)
        self.TT("dve", modT, mpv, self.adab[:, i, :].unsqueeze(2).to_broadcast([128, 72, 2]), ALU.add,
                r=[mps, self.S_const], w=[S_mod])
        for sub in range(3):
            g = self.normg[:, i, sub, :].unsqueeze(2).to_broadcast([128, KC, 2])
            self.STT("dve", Asc[:, sub], modT[:, (3 * sub + 1) * 8:(3 * sub + 2) * 8, :], 1.0, g, ALU.add, ALU.mult,
                     r=[S_mod, self.S_const], w=[S_sc])
            self.TS("dve", Gsc[:, sub], modT[:, (3 * sub + 2) * 8:(3 * sub + 3) * 8, :], 0.5 if sub != 1 else 1.0, None,
                    ALU.mult, None, r=[S_mod], w=[S_sc])

    def ada(self, i):
        self.phase()
        for pc in range(12):
            s = self.ws_i % 2
            self.ws_i += 1
            self.ada_piece(i, pc, self.WS[s], self.WSs[s], f"ws{s}")
        self.ada_finish(i)

    def rstd_from_ps(self, ssp, sss, n, inv_n, tmp, tmps):
        self.ACT(tmp[:, :n], ssp[:, :n], AF.Ln, r=[sss, self.S_const], w=[tmps], bias=self.epsb[:, 0:1], scale=inv_n)
        self.ACT(tmp[:, :n], tmp[:, :n], AF.Exp, r=[tmps], w=[tmps], scale=-0.5)

    def norm_bufs(self, one_bank=False):
        P = self.P
        return dict(SQ=Rot([(self.abuf([128, 512], BF16), P.slot("sq")) for _ in range(2)]),
                    RS=Rot([(self.abuf([128, 512], F32), P.slot("rs")) for _ in range(2)]),
                    TM=Rot([(self.abuf([128, 512], F32), P.slot("tm")) for _ in range(3)]),
                    PSr=Rot([(self.ps[6], self.PS[6])] + ([] if one_bank else [(self.ps[7], self.PS[7])])))

    def norm_group(self, nb, sub, g):
        c0, n, lc = GROUPS[g]
        ssp, sss = nb["PSr"].next()
        for c in range(KC):
            sq, sqs = nb["SQ"].next()
            self.ACT(sq[:, :n], self.hT[:, c, c0:c0 + n], AF.Square, r=[self.HS[c][g]], w=[sqs])
            self.MM(ssp[:, :n], self.ones_bf[:], sq[:, :n], c == 0, c == KC - 1, r=[sqs, self.S_const], w=[sss])
        rs, rss = nb["RS"].next()
        self.rstd_from_ps(ssp, sss, n, 1.0 / D, rs, rss)
        for c in range(KC):
            tm, tms = nb["TM"].next()
            self.STT("dve", tm[:, :n], self.hT[:, c, c0:c0 + n], self.Asc[:, sub, c, lc:lc + 1], rs[:, :n], ALU.mult, ALU.mult,
                     r=[self.HS[c][g], self.S_sc, rss], w=[tms])
            self.ACT(self.UT[:, c, c0:c0 + n], tm[:, :n], AF.Identity, r=[tms, self.S_mod], w=[self.US[c][g]],
                     bias=self.modT[:, 3 * sub * 8 + c, lc:lc + 1], scale=1.0)

    def norm_mod(self, i, sub, ctx=True):
        self.phase()
        nb = self.norm_bufs()
        for g, (c0, n, lc) in enumerate(GROUPS):
            if lc and not ctx:
                continue
            self.norm_group(nb, sub, g)

    def ffn(self, i, f, sub, ctx=True, ada_next=None):
        self.phase()
        P = self.P
        w13 = self.dram["ffn_w13"][i, f].rearrange("(kc p) n -> p kc n", p=128)
        w2 = self.dram["ffn_w2"][i, f].rearrange("(j p) n -> p j n", p=128)
        W2 = [self.abuf([128, 3, D], BF16) for _ in range(2)]
        W2s = [P.slot(f"w2s{k}") for k in range(2)]
        SG = Rot([(self.abuf([128, 512], F32), P.slot("sg")) for _ in range(2)])
        AB = Rot([[(self.abuf([128, 512], BF16), P.slot("ab")) for _ in range(3)] for _ in range(2)])
        PG = Rot([(self.ps[0], self.PS[0]), (self.ps[1], self.PS[1])])
        PU = Rot([(self.ps[2], self.PS[2]), (self.ps[3], self.PS[3])])
        PY = Rot([(self.ps[4], self.PS[4]), (self.ps[5], self.PS[5])])
        pieces = [(0, 3), (3, 3), (6, 3), (9, 3), (12, 3), (15, 3), (18, 2), (20, 2)]
        nb = self.norm_bufs(one_bank=ada_next is not None)
        if ada_next is not None:
            ADW = self.abuf([128, KC, 768], BF16)
            ADWs = P.slot("adw")
        active = [g for g, (c0, n, lc) in enumerate(GROUPS) if not (lc and not ctx)]
        for pi, (j0, npr) in enumerate(pieces):
            s = self.ws_i % 2
            self.ws_i += 1
            WSb, WSs = self.WS[s], self.WSs[s]
            self.DMA("pool", WSb[:, :, 0:npr * 128], w13[:, :, j0 * 128:(j0 + npr) * 128], f"ws{s}", w=[WSs], nowaw=True)
            self.DMA("pool", WSb[:, :, 384:384 + npr * 128], w13[:, :, DFF + j0 * 128:DFF + (j0 + npr) * 128], f"ws{s}", w=[WSs], nowaw=True)
            self.DMA("pool", W2[s][:, 0:npr, :], w2[:, j0:j0 + npr, :], f"w2s{s}", w=[W2s[s]])
            for gi, g in enumerate(active):
                c0, n, lc = GROUPS[g]
                if pi == 0:
                    if gi == 0:
                        self.norm_group(nb, sub, g)
                    if gi + 1 < len(active):
                        self.norm_group(nb, sub, active[gi + 1])
                ab = AB.next()
                for jj in range(npr):
                    pg, pgs = PG.next()
                    pu, pus = PU.next()
                    for kc in range(KC):
                        self.MM(pg[:, :n], WSb[:, kc, jj * 128:(jj + 1) * 128], self.UT[:, kc, c0:c0 + n], kc == 0, kc == KC - 1,
                                r=[WSs, self.US[kc][g]], w=[pgs])
                    for kc in range(KC):
                        self.MM(pu[:, :n], WSb[:, kc, 384 + jj * 128:384 + (jj + 1) * 128], self.UT[:, kc, c0:c0 + n], kc == 0, kc == KC - 1,
                                r=[WSs, self.US[kc][g]], w=[pus])
                    sg, sgs = SG.next()
                    self.ACT(sg[:, :n], pg[:, :n], AF.Silu, r=[pgs], w=[sgs])
                    self.TT("dve", ab[jj][0][:, :n], sg[:, :n], pu[:, :n], ALU.mult, r=[sgs, pus], w=[ab[jj][1]])
                for c in range(KC):
                    py, pys = PY.next()
                    for jj in range(npr):
                        self.MM(py[:, :n], W2[s][:, jj, c * 128:(c + 1) * 128], ab[jj][0][:, :n], jj == 0, jj == npr - 1,
                                r=[W2s[s], ab[jj][1]], w=[pys])
                    self.STT("dve", self.hT[:, c, c0:c0 + n], py[:, :n], self.Gsc[:, sub, c, lc:lc + 1], self.hT[:, c, c0:c0 + n],
                             ALU.mult, ALU.add, r=[pys, self.S_sc, self.HS[c][g]], w=[self.HS[c][g]])
                if ada_next is not None and gi < 2:
                    pc = pi * 12 // 8 + gi
                    if pc < (pi + 1) * 12 // 8:
                        self.ada_piece(ada_next, pc, ADW, ADWs, "adw")
        if ada_next is not None:
            self.ada_finish(ada_next)

    def wo_partial(self, wo_name, chunks, RTs, RTslots, WOb, WOs, PY, ctx=True, key=None):
        wo = self.dram[wo_name].rearrange("(kc p) n -> p kc n", p=128)
        nk = len(chunks)
        for q, kc in enumerate(chunks):
            self.DMA("pool", WOb[:, q, :], wo[:, kc, :], key or f"wo{q}", w=[WOs[q]], nowaw=key is not None)
        for g, (c0, n, lc) in enumerate(GROUPS):
            if lc and not ctx:
                continue
            for c in range(KC):
                py, pys = PY.next()
                for q in range(nk):
                    self.MM(py[:, :n], WOb[:, q, c * 128:(c + 1) * 128], RTs[q][:, c0:c0 + n], q == 0, q == nk - 1,
                            r=[WOs[q], RTslots[q][g]], w=[pys])
                self.STT("dve", self.hT[:, c, c0:c0 + n], py[:, :n], self.Gsc[:, 1, c, lc:lc + 1], self.hT[:, c, c0:c0 + n],
                         ALU.mult, ALU.add, r=[pys, self.S_sc, self.HS[c][g]], w=[self.HS[c][g]])

    def mix_na(self, i):
        self.phase()
        P = self.P
        wkvq = self.dram["na_w_kvq"].rearrange("(kc p) n -> p kc n", p=128)
        nab = self.dram["na_bias"]
        KTz = [self.abuf([128, T], BF16) for _ in range(2)]
        QT = self.abuf([128, T], BF16)
        VA = [self.abuf([128, 18, 128], BF16) for _ in range(2)]
        OT = self.abuf([128, T], BF16)
        KTs = [P.slot("kt") for _ in range(5)]
        QTs = [P.slot("qt") for _ in range(5)]
        VAs = [P.slot("va0"), P.slot("va1")]
        OTs = [P.slot(f"ot{g}") for g in range(5)]
        BT = Rot([(self.abuf([128, 512], BF16), P.slot("bt"), k) for k in range(4)])
        PT = Rot([(self.abuf([128, 512], BF16), P.slot("pt")) for _ in range(5)])
        RD = Rot([(self.abuf([128, 512], F32), P.slot("rd")) for _ in range(2)])
        WOb = self.abuf([128, 1, D], BF16)
        WOs = [P.slot("wo")]
        PSS = Rot([(self.ps[k], self.PS[k]) for k in (0, 1, 2)])
        PO = Rot([(self.ps[3], self.PS[3]), (self.ps[4], self.PS[4])])
        PP = Rot([(self.ps[k], self.PS[k]) for k in (5, 6, 7)])
        PY = PP
        self.MEMSET("pool", VA[0][:, :, 64:128], 1.0, w=[VAs[0]])
        self.MEMSET("pool", VA[1][:, :, 0:64], 1.0, w=[VAs[1]])
        self.MEMSET("pool", KTz[0][64:128, :], 0.0, w=KTs)
        self.MEMSET("pool", KTz[1][0:64, :], 0.0, w=KTs)
        local_tiles = {0: [(2 + a // 2, k) for k, a in enumerate(range(0, 12, 2))],
                       1: [(2 + a // 2, 6 + k) for k, a in enumerate(range(4, 20, 2))],
                       2: [(2 + a // 2, 6 + k) for k, a in enumerate(range(12, 28, 2))],
                       3: [(2 + a // 2, 14 + k) for k, a in enumerate(range(20, 32, 2))]}
        for hp in range(8):
            s = self.ws_i % 2
            self.ws_i += 1
            WSb, WSs = self.WS[s], self.WSs[s]
            for k3 in range(3):
                self.DMA("pool", WSb[:, :, k3 * 128:(k3 + 1) * 128], wkvq[:, :, k3 * D + hp * 128:k3 * D + (hp + 1) * 128],
                         f"ws{s}", w=[WSs], nowaw=True)
            for g, (c0, n, lc) in enumerate(GROUPS):
                for (dsts, wc, sc) in ((KTs, 0, 1.0), (QTs, 2, 0.125)):
                    pp, pps = PP.next()
                    for kc in range(KC):
                        self.MM(pp[:, :n], WSb[:, kc, wc * 128:(wc + 1) * 128], self.UT[:, kc, c0:c0 + n], kc == 0, kc == KC - 1,
                                r=[WSs, self.US[kc][g]], w=[pps])
                    if wc == 0:
                        self.ACT(KTz[0][0:64, c0:c0 + n], pp[0:64, :n], AF.Identity, r=[pps], w=[dsts[g]], scale=sc)
                        self.ACT(KTz[1][64:128, c0:c0 + n], pp[64:128, :n], AF.Identity, r=[pps], w=[dsts[g]], scale=sc)
                    else:
                        self.ACT(QT[:, c0:c0 + n], pp[:, :n], AF.Identity, r=[pps], w=[dsts[g]], scale=sc)
            for t4 in range(0, 18, 4):
                nt = min(4, 18 - t4)
                pp, pps = PP.next()
                for q in range(nt):
                    tt = t4 + q
                    g = 0 if tt < 2 else 1 + (tt - 2) // 4
                    for kc in range(KC):
                        self.MM(pp[:, q * 128:(q + 1) * 128], self.UT[:, kc, tt * 128:(tt + 1) * 128], WSb[:, kc, 128:256], kc == 0, kc == KC - 1,
                                r=[WSs, self.US[kc][g]], w=[pps])
                ppv = pp[:, 0:nt * 128].rearrange("p (a b) -> p a b", b=128)
                self.CP("dve", VA[0][:, t4:t4 + nt, 0:64], ppv[:, :, 0:64], r=[pps], w=[VAs[0]])
                self.CP("act", VA[1][:, t4:t4 + nt, 64:128], ppv[:, :, 64:128], r=[pps], w=[VAs[1]])
            items = []
            for hh in range(2):
                for g, (c0, n, lc) in enumerate(GROUPS):
                    tiles = [(0, None), (1, None)]
                    if not lc:
                        tiles = local_tiles[g - 1] + tiles
                    for ti, (kt, bi) in enumerate(tiles):
                        items.append(dict(hh=hh, g=g, c0=c0, n=n, kt=kt, bi=bi, first=ti == 0, last=ti == len(tiles) - 1))
            cur = {}

            def st_a(it):
                hh, g, c0, n, kt, bi = it["hh"], it["g"], it["c0"], it["n"], it["kt"], it["bi"]
                hb = hh * 64
                sp_, sps = PSS.next()
                kg = 0 if kt < 2 else 1 + (kt - 2) // 4
                if bi is not None:
                    bt, bts, bk = BT.next()
                    self.DMA("pool", bt[:], nab[2 * hp + hh, bi], f"nb{bk}", w=[bts])
                self.MM(sp_[:, :n], KTz[hh][:, kt * 128:(kt + 1) * 128], QT[:, c0:c0 + n], True, bi is None,
                        r=[KTs[kg], QTs[g]], w=[sps])
                if bi is not None:
                    self.MM(sp_[:, :n], self.ident_bf[:], bt[:, :n], False, True, r=[bts, self.S_const], w=[sps])
                pt, pts = PT.next()
                self.ACT(pt[:, :n], sp_[:, :n], AF.Exp, r=[sps], w=[pts])
                it["pt"] = (pt, pts)

            def st_b(it):
                hh, g, c0, n, kt = it["hh"], it["g"], it["c0"], it["n"], it["kt"]
                hb = hh * 64
                db = 64 - hb
                if it["first"]:
                    cur["po"] = PO.next()
                po, pos = cur["po"]
                pt, pts = it["pt"]
                self.MM(po[:, :n], VA[hh][:, kt, :], pt[:, :n], it["first"], it["last"], r=[VAs[hh], pts], w=[pos])
                if it["last"]:
                    rd, rds = RD.next()
                    self.ACT(rd[db:db + 64, :n], po[db:db + 64, :n], AF.Ln, r=[pos], w=[rds])
                    self.ACT(rd[db:db + 64, :n], rd[db:db + 64, :n], AF.Exp, r=[rds], w=[rds], scale=-1.0)
                    self.CP("dve", rd[hb:hb + 64, :n], rd[db:db + 64, :n], r=[rds], w=[rds])
                    self.TT("dve", OT[hb:hb + 64, c0:c0 + n], po[hb:hb + 64, :n], rd[hb:hb + 64, :n], ALU.mult, r=[pos, rds], w=[OTs[g]])

            pipeline(items, st_a, st_b, look=2)
            self.wo_partial("na_w_o", [hp], [OT], [OTs], WOb, WOs, PY)

    def subphase(self, off):
        self.ar_off = off
        self.P.barrier()

    def mix_ml(self, i):
        self.phase()
        P = self.P
        win = self.dram["ml_w_in"].rearrange("(kc p) n -> p kc n", p=128)
        WG = self.abuf([128, KC, 16], BF16)
        gb = self.abuf([128, 16], F32)
        cv = self.abuf([128, 16, 3], F32)
        ng = self.abuf([128, KC], F32)
        GT = self.abuf([128, 18, 16], F32)
        IG = self.abuf([128, 18, 8], F32)
        LF = self.abuf([128, 18, 8], F32)
        KBS = self.abuf([128, 18, 8], F32)
        BTK = self.abuf([128, 18, 8], F32)
        onesf = self.abuf([128, 128], F32)
        onec = self.abuf([128, 1], F32)
        S_set = P.slot("mlset")
        S_g = P.slot("mlgates")
        KTh = [self.abuf([128, T], BF16) for _ in range(2)]
        QTh = [self.abuf([128, T], BF16) for _ in range(2)]
        VT = self.abuf([128, 18, 256], BF16)
        RTh = [self.abuf([128, T], BF16) for _ in range(2)]
        KTs, QTs, VTs = P.slot("kth"), P.slot("qth"), P.slot("vth")
        RTs = [[P.slot(f"rt{cc}_{g}") for g in range(5)] for cc in range(2)]
        off_sub = self.ar_off
        PSS = Rot([(self.ps[k], self.PS[k]) for k in (0, 1, 2)])
        NUM = [(self.ps[3], self.PS[3]), (self.ps[4], self.PS[4])]
        DEN = (self.ps[5], self.PS[5])
        PP = Rot([(self.ps[6], self.PS[6]), (self.ps[7], self.PS[7])])
        PY = PP
        cmf = self.cmask[:, 0, :]
        cmb = self.cmask[:, 1, :]
        self.DMA("pool", WG[:], win[:, :, 2048:2064], "misc", w=[S_set])
        self.DMA("sp", gb[:], self.dram["ml_gate_b"], "spm1", w=[S_set])
        self.DMA("sp", cv[:], self.dram["ml_conv"], "spm2", w=[S_set])
        self.DMA("sp", ng[:], self.dram["ml_norm_g"], "spm3", w=[S_set])
        self.MEMSET("dve", onesf[:], 1.0, w=[S_g])
        self.MEMSET("dve", onec[:], 1.0, w=[S_g])
        pp, pps = PP.next()
        for tt in range(18):
            g = 0 if tt < 2 else 1 + (tt - 2) // 4
            for kc in range(KC):
                self.MM(pp[:, tt * 16:(tt + 1) * 16], self.UT[:, kc, tt * 128:(tt + 1) * 128], WG[:, kc, :], kc == 0, kc == KC - 1,
                        r=[S_set, self.US[kc][g]], w=[pps])
        self.TT("dve", GT[:], pp[:, 0:288].rearrange("p (a b) -> p a b", b=16), gb[:, :].unsqueeze(1).to_broadcast([128, 18, 16]), ALU.add,
                r=[pps, S_set], w=[S_g])
        for d in range(2):
            self.CP("dve", IG[:, :, d * 4:(d + 1) * 4], GT[:, :, d * 8:d * 8 + 4], r=[S_g], w=[S_g])
            self.ACT(LF[:, :, d * 4:(d + 1) * 4], GT[:, :, d * 8 + 4:d * 8 + 8], AF.Exp, r=[S_g], w=[S_g], scale=-1.0)
        self.ACT(LF[:], LF[:], AF.Ln, r=[S_g], w=[S_g], bias=onec[:, 0:1], scale=1.0)
        self.TS("dve", LF[:], LF[:], -1.0, None, ALU.mult, None, r=[S_g], w=[S_g])
        ptot, ptots = PP.next()
        for i_ in range(18):
            self.MM(ptot[:, i_ * 8:(i_ + 1) * 8], onesf[:], LF[:, i_, :], True, True, r=[S_g], w=[ptots])
        TOT = GT[:, :, 0:8]
        PRE = GT[:, :, 8:16]
        self.CP("dve", TOT, ptot[:, 0:144].rearrange("p (a b) -> p a b", b=8), r=[ptots, S_g], w=[S_g])
        self.MEMSET("dve", PRE[:, 0, 0:4], 0.0, w=[S_g])
        for n_ in range(1, 18):
            self.TT("dve", PRE[:, n_, 0:4], PRE[:, n_ - 1, 0:4], TOT[:, n_ - 1, 0:4], ALU.add, r=[S_g], w=[S_g])
        self.MEMSET("dve", PRE[:, 1, 4:8], 0.0, w=[S_g])
        self.CP("dve", PRE[:, 0, 4:8], TOT[:, 1, 4:8], r=[S_g], w=[S_g])
        self.TT("dve", PRE[:, 17, 4:8], TOT[:, 0, 4:8], TOT[:, 1, 4:8], ALU.add, r=[S_g], w=[S_g])
        for n_ in range(16, 1, -1):
            self.TT("dve", PRE[:, n_, 4:8], PRE[:, n_ + 1, 4:8], TOT[:, n_ + 1, 4:8], ALU.add, r=[S_g], w=[S_g])
        pb, pbs = PP.next()
        for n_ in range(18):
            self.MM(pb[:, n_ * 8:n_ * 8 + 4], cmf[:, 384:512], LF[:, n_, 0:4], True, True, r=[S_g, self.S_const], w=[pbs])
            self.MM(pb[:, n_ * 8 + 4:n_ * 8 + 8], cmb[:, 384:512], LF[:, n_, 4:8], True, True, r=[S_g, self.S_const], w=[pbs])
        pbv = pb[:, 0:144].rearrange("p (a b) -> p a b", b=8)
        self.TT("dve", BTK[:], pbv, PRE, ALU.add, r=[pbs, S_g], w=[S_g])
        self.STT("dve", KBS[:], IG[:], -4.0 * math.log(2.0), BTK[:], ALU.add, ALU.subtract, r=[S_g], w=[S_g])
        for h in range(4):
            self.subphase(off_sub)
            PK = self.abuf([128, T], F32)
            ACC = self.abuf([128, T], F32)
            PKs, ACCs = P.slot("pk"), P.slot("acc")
            s = self.ws_i % 2
            s2 = (self.ws_i + 1) % 2
            self.ws_i += 2
            WSb, WSs = self.WS[s], self.WSs[s]
            for k3, c_src in enumerate((h * 256, 1024 + h * 256, 2064 + h * 256)):
                self.DMA("pool", WSb[:, :, k3 * 256:(k3 + 1) * 256], win[:, :, c_src:c_src + 256], f"ws{s}", w=[WSs], nowaw=True)
            self.DMA("pool", self.WS[s2][:, :, 0:256], win[:, :, 3088 + h * 256:3088 + (h + 1) * 256], f"ws{s2}", w=[self.WSs[s2]], nowaw=True)
            for (dst, dsts, wc0, cvb) in ((KTh, KTs, 0, 0), (QTh, QTs, 512, 8)):
                for cc in range(2):
                    ch = cvb + 2 * h + cc
                    for g, (c0, n, lc) in enumerate(GROUPS):
                        pp, pps = PP.next()
                        for kc in range(KC):
                            self.MM(pp[:, :n], WSb[:, kc, wc0 + cc * 128:wc0 + (cc + 1) * 128], self.UT[:, kc, c0:c0 + n], kc == 0, kc == KC - 1,
                                    r=[WSs, self.US[kc][g]], w=[pps])
                        self.CP("act", PK[:, c0:c0 + n], pp[:, :n], r=[pps], w=[PKs])
                    self.TS("dve", ACC[:], PK[:], cv[:, ch, 1:2], None, ALU.mult, None, r=[PKs, S_set], w=[ACCs])
                    for (a, b) in ((0, NCTX), (NCTX, T)):
                        self.STT("dve", ACC[:, a + 1:b], PK[:, a:b - 1], cv[:, ch, 0:1], ACC[:, a + 1:b], ALU.mult, ALU.add,
                                 r=[PKs, S_set, ACCs], w=[ACCs])
                        self.STT("dve", ACC[:, a:b - 1], PK[:, a + 1:b], cv[:, ch, 2:3], ACC[:, a:b - 1], ALU.mult, ALU.add,
                                 r=[PKs, S_set, ACCs], w=[ACCs])
                    self.ACT(dst[cc][:], ACC[:], AF.Silu, r=[ACCs], w=[dsts])
            self.proj_v_tok(WSb, WSs, 256, 256, VT, VTs, PP)
            for cc in range(2):
                for g, (c0, n, lc) in enumerate(GROUPS):
                    pp, pps = PP.next()
                    for kc in range(KC):
                        self.MM(pp[:, :n], self.WS[s2][:, kc, cc * 128:(cc + 1) * 128], self.UT[:, kc, c0:c0 + n], kc == 0, kc == KC - 1,
                                r=[self.WSs[s2], self.US[kc][g]], w=[pps])
                    self.ACT(RTh[cc][:, c0:c0 + n], pp[:, :n], AF.Sigmoid, r=[pps], w=[RTs[cc][g]])
            self.subphase(off_sub)
            BREP = Rot([(self.abuf([128, 512], F32), P.slot("brep")) for _ in range(2)])
            DT = Rot([(self.abuf([128, 512], F32), P.slot("dt")) for _ in range(3)])
            AT = Rot([(self.abuf([128, 512], BF16), P.slot("at")) for _ in range(3)])
            AD = (self.abuf([128, 512], F32), P.slot("ad"))
            HS_ = [(self.abuf([128, 512], F32), P.slot(f"hsum{cc}")) for cc in range(2)]
            TMP = (self.abuf([128, 512], F32), P.slot("tmp"))
            SQ = Rot([(self.abuf([128, 512], BF16), P.slot("sq")) for _ in range(2)])
            items = []
            for g, (c0, n, lc) in enumerate(GROUPS):
                n0, m = c0 // 128, n // 128
                for d in range(2):
                    cm = cmf if d == 0 else cmb
                    diag = [(n0 + j, cm[:, 384 - 128 * j:384 - 128 * j + n]) for j in range(m)]
                    if d == 0:
                        tiles = [(i_, None) for i_ in range(n0)] + diag
                    elif lc:
                        tiles = diag
                    else:
                        tiles = [(0, None), (1, None)] + diag + [(i_, None) for i_ in range(n0 + m, 18)]
                    for ti, (it_, mk) in enumerate(tiles):
                        items.append(dict(g=g, c0=c0, n=n, n0=n0, m=m, d=d, it=it_, mk=mk, first=ti == 0, last=ti == len(tiles) - 1))
            cur = {}

            def st_a(it, h=h):
                g, c0, n, d, tl, mk = it["g"], it["c0"], it["n"], it["d"], it["it"], it["mk"]
                hd = d * 4 + h
                if it["first"]:
                    pb_, pbs_ = PP.next()
                    for q in range(it["m"]):
                        self.MM(pb_[:, q * 128:(q + 1) * 128], BTK[:, it["n0"] + q, hd:hd + 1].to_broadcast([128, 128]), self.ident[:], True, True,
                                r=[S_g, self.S_const], w=[pbs_])
                    cur["brep"] = BREP.next()
                    self.CP("act", cur["brep"][0][:, :n], pb_[:, :n], r=[pbs_], w=[cur["brep"][1]])
                brep, breps = cur["brep"]
                sp_, sps = PSS.next()
                for cc in range(2):
                    self.MM(sp_[:, :n], KTh[cc][:, tl * 128:(tl + 1) * 128], QTh[cc][:, c0:c0 + n], cc == 0, cc == 1, r=[KTs, QTs], w=[sps])
                dt, dts = DT.next()
                self.ACT(dt[:, :n], brep[:, :n], AF.Exp, r=[breps, S_g], w=[dts], bias=KBS[:, tl, hd:hd + 1], scale=1.0)
                if mk is not None:
                    self.TT("pool", dt[:, :n], dt[:, :n], mk, ALU.mult, r=[dts, self.S_const], w=[dts])
                at, ats = AT.next()
                self.TT("dve", at[:, :n], sp_[:, :n], dt[:, :n], ALU.mult, r=[sps, dts], w=[ats])
                it["at"] = (at, ats)

            def st_b(it, h=h):
                g, c0, n, d, tl = it["g"], it["c0"], it["n"], it["d"], it["it"]
                at, ats = it["at"]
                den, dens = DEN
                for cc in range(2):
                    self.MM(NUM[cc][0][:, :n], VT[:, tl, cc * 128:(cc + 1) * 128], at[:, :n], it["first"], it["last"], r=[VTs, ats], w=[NUM[cc][1]])
                self.MM(den[:, :n], self.ones_bf[:], at[:, :n], it["first"], it["last"], r=[self.S_const, ats], w=[dens])
                if not it["last"]:
                    return
                ad, ads = AD
                self.ACT(ad[:, :n], den[:, :n], AF.Abs, r=[dens], w=[ads])
                self.ACT(ad[:, :n], ad[:, :n], AF.Ln, r=[ads], w=[ads])
                self.TS("dve", ad[:, :n], ad[:, :n], 0.0, None, ALU.max, None, r=[ads], w=[ads])
                self.ACT(ad[:, :n], ad[:, :n], AF.Exp, r=[ads], w=[ads], scale=-1.0)
                for cc in range(2):
                    hs, hss = HS_[cc]
                    if d == 0:
                        self.TT("dve", hs[:, :n], NUM[cc][0][:, :n], ad[:, :n], ALU.mult, r=[NUM[cc][1], ads], w=[hss])
                    else:
                        tm, tms = TMP
                        self.TT("dve", tm[:, :n], NUM[cc][0][:, :n], ad[:, :n], ALU.mult, r=[NUM[cc][1], ads], w=[tms])
                        self.TT("dve", hs[:, :n], hs[:, :n], tm[:, :n], ALU.add, r=[hss, tms], w=[hss])
                if d == 0:
                    return
                pp, pps = PP.next()
                for cc in range(2):
                    sq, sqs = SQ.next()
                    self.ACT(sq[:, :n], HS_[cc][0][:, :n], AF.Square, r=[HS_[cc][1]], w=[sqs])
                    self.MM(pp[:, :n], self.ones_bf[:], sq[:, :n], cc == 0, cc == 1, r=[sqs, self.S_const], w=[pps])
                self.rstd_from_ps(pp, pps, n, 1.0 / 256, ad, ads)
                for cc in range(2):
                    tm, tms = TMP
                    self.STT("dve", tm[:, :n], HS_[cc][0][:, :n], ng[:, 2 * h + cc:2 * h + cc + 1], ad[:, :n], ALU.mult, ALU.mult,
                             r=[HS_[cc][1], S_set, ads], w=[tms])
                    self.TT("dve", RTh[cc][:, c0:c0 + n], RTh[cc][:, c0:c0 + n], tm[:, :n], ALU.mult, r=[RTs[cc][g], tms], w=[RTs[cc][g]])

            pipeline(items, st_a, st_b, look=2)
            WOb = self.WS[s2][:, 0:3, :].rearrange("p a b -> p (a b)")[:, 0:2 * D].rearrange("p (a b) -> p a b", b=D)
            WOs = [self.WSs[s2], self.WSs[s2]]
            self.wo_partial("ml_w_o", [2 * h, 2 * h + 1], RTh, RTs, WOb, WOs, PY, key=f"ws{s2}")

    def proj_v_tok(self, WSb, WSs, wc0, ncols, VT, VTs, PP):
        per = 512 // ncols
        for t4 in range(0, 18, per):
            nt = min(per, 18 - t4)
            pp, pps = PP.next()
            for q in range(nt):
                tt = t4 + q
                g = 0 if tt < 2 else 1 + (tt - 2) // 4
                for kc in range(KC):
                    self.MM(pp[:, q * ncols:(q + 1) * ncols], self.UT[:, kc, tt * 128:(tt + 1) * 128], WSb[:, kc, wc0:wc0 + ncols],
                            kc == 0, kc == KC - 1, r=[WSs, self.US[kc][g]], w=[pps])
            ppv = pp[:, 0:nt * ncols].rearrange("p (a b) -> p a b", b=ncols)
            self.CP("act", VT[:, t4:t4 + nt, :], ppv, r=[pps], w=[VTs])

    def mix_da(self, i):
        self.phase()
        P = self.P
        lam_init = 0.8 - 0.6 * math.exp(-0.3 * i)
        wkvq = self.dram["da_w_kvq"].rearrange("(kc p) n -> p kc n", p=128)
        wrot = self.dram["da_w_rot"].rearrange("(kc p) n -> p kc n", p=128)
        rope = self.dram["rope"]
        KTz = [self.abuf([128, T], BF16) for _ in range(2)]
        QT = self.abuf([128, T], BF16)
        VT = self.abuf([128, 18, 128], BF16)
        RT = self.abuf([128, T], BF16)
        KTs = [P.slot("kt") for _ in range(5)]
        QTs = [P.slot("qt") for _ in range(5)]
        VTs = P.slot("vt")
        RTs = [P.slot(f"rt{g}") for g in range(5)]
        CSb = Rot([(self.abuf([128, 2, 512], F32), P.slot("cs"), k) for k in range(2)])
        T1 = Rot([(self.abuf([128, 512], F32), P.slot("t1")) for _ in range(2)])
        T2 = Rot([(self.abuf([128, 512], F32), P.slot("t2")) for _ in range(2)])
        PT = Rot([(self.abuf([128, 512], BF16), P.slot("pt")) for _ in range(5)])
        R0 = (self.abuf([128, 512], F32), P.slot("r0"))
        R1 = (self.abuf([128, 512], F32), P.slot("r1"))
        OB = (self.abuf([128, 512], F32), P.slot("ob"))
        SQ = (self.abuf([128, 512], BF16), P.slot("sq"))
        RS = (self.abuf([128, 512], F32), P.slot("rs"))
        lamb = self.abuf([128, 4, 64], F32)
        lamt = self.abuf([128, 2, 64], F32)
        lams = self.abuf([128, 4], F32)
        ngb = self.abuf([128, KC], F32)
        S_l = P.slot("lam")
        WOb = self.abuf([128, 1, D], BF16)
        WOs = [P.slot("wo")]
        PSS = Rot([(self.ps[k], self.PS[k]) for k in (0, 1, 2)])
        ACC = Rot([((self.ps[3], self.PS[3]), (self.ps[4], self.PS[4])), ((self.ps[5], self.PS[5]), (self.ps[6], self.PS[6]))])
        PN = (self.ps[7], self.PS[7])
        PP = Rot([(self.ps[k], self.PS[k]) for k in (7, 0, 1, 2)])
        PY = PP
        self.MEMSET("pool", KTz[0][64:128, :], 0.0, w=KTs)
        self.MEMSET("pool", KTz[1][0:64, :], 0.0, w=KTs)
        self.DMA("sp", lamb[:], self.dram["da_lam"], "spm1", w=[S_l])
        self.DMA("sp", ngb[:], self.dram["da_norm_g"], "spm2", w=[S_l])
        for q in range(2):
            self.TT("dve", lamt[:, q, :], lamb[:, 2 * q, :], lamb[:, 2 * q + 1, :], ALU.mult, r=[S_l], w=[S_l])
            self.P.op("dve", lambda e, q=q: e.reduce_sum(lams[:, q:q + 1], lamt[:, q, :], axis=AX.X), r=[S_l], w=[S_l])
        self.ACT(lams[:, 0:2], lams[:, 0:2], AF.Exp, r=[S_l], w=[S_l])
        self.STT("dve", lams[:, 2:3], lams[:, 1:2], -lam_init, lams[:, 0:1], ALU.add, ALU.subtract, r=[S_l], w=[S_l])
        self.TS("dve", ngb[:], ngb[:], 1.0 - lam_init, None, ALU.mult, None, r=[S_l], w=[S_l])
        neglam = lams[:, 2:3]
        for h in range(8):
            s = self.ws_i % 2
            self.ws_i += 1
            WSb, WSs = self.WS[s], self.WSs[s]
            for k3 in range(3):
                self.DMA("pool", WSb[:, :, k3 * 128:(k3 + 1) * 128], wkvq[:, :, k3 * D + h * 128:k3 * D + (h + 1) * 128],
                         f"ws{s}", w=[WSs], nowaw=True)
            for k2 in range(2):
                self.DMA("pool", WSb[:, :, 384 + k2 * 128:384 + (k2 + 1) * 128], wrot[:, :, k2 * D + h * 128:k2 * D + (h + 1) * 128],
                         f"ws{s}", w=[WSs], nowaw=True)
            for g, (c0, n, lc) in enumerate(GROUPS):
                if not lc:
                    cs_, css, ck = CSb.next()
                    self.DMA("sp", cs_[:], rope[:, :, c0 - NCTX:c0 - NCTX + n], f"cs{ck}", w=[css])
                for (dst, dsts, wc, rc, sc) in ((None, KTs, 0, 384, 1.0), (QT, QTs, 256, 512, 0.125)):
                    pp, pps = PP.next()
                    for kc in range(KC):
                        self.MM(pp[:, :n], WSb[:, kc, wc:wc + 128], self.UT[:, kc, c0:c0 + n], kc == 0, kc == KC - 1,
                                r=[WSs, self.US[kc][g]], w=[pps])
                    if lc:
                        if dst is None:
                            self.ACT(KTz[0][0:64, c0:c0 + n], pp[0:64, :n], AF.Identity, r=[pps], w=[dsts[g]], scale=sc)
                            self.ACT(KTz[1][64:128, c0:c0 + n], pp[64:128, :n], AF.Identity, r=[pps], w=[dsts[g]], scale=sc)
                        else:
                            self.ACT(dst[:, c0:c0 + n], pp[:, :n], AF.Identity, r=[pps], w=[dsts[g]], scale=sc)
                        continue
                    pr, prs = PP.next()
                    for kc in range(KC):
                        self.MM(pr[:, :n], WSb[:, kc, rc:rc + 128], self.UT[:, kc, c0:c0 + n], kc == 0, kc == KC - 1,
                                r=[WSs, self.US[kc][g]], w=[prs])
                    t1, t1s = T1.next()
                    t2, t2s = T2.next()
                    self.STT("dve", t1[:, :n], pp[:, :n], sc, cs_[:, 0, :n], ALU.mult, ALU.mult, r=[pps, css], w=[t1s])
                    self.STT("dve", t2[:, :n], pr[:, :n], sc, cs_[:, 1, :n], ALU.mult, ALU.mult, r=[prs, css], w=[t2s])
                    if dst is None:
                        self.TT("pool", KTz[0][0:64, c0:c0 + n], t1[0:64, :n], t2[0:64, :n], ALU.add, r=[t1s, t2s], w=[dsts[g]])
                        self.TT("pool", KTz[1][64:128, c0:c0 + n], t1[64:128, :n], t2[64:128, :n], ALU.add, r=[t1s, t2s], w=[dsts[g]])
                    else:
                        self.TT("pool", dst[:, c0:c0 + n], t1[:, :n], t2[:, :n], ALU.add, r=[t1s, t2s], w=[dsts[g]])
            self.proj_v_tok(WSb, WSs, 128, 128, VT, VTs, PP)
            items = []
            for g, (c0, n, lc) in enumerate(GROUPS):
                tiles = [0, 1] if lc else list(range(18))
                for j in range(2):
                    for ti, kt in enumerate(tiles):
                        items.append(dict(g=g, c0=c0, n=n, j=j, kt=kt, first=ti == 0, last=ti == len(tiles) - 1))
            cur = {}

            def st_a(it):
                g, c0, n, j, kt = it["g"], it["c0"], it["n"], it["j"], it["kt"]
                jb = j * 64
                sp_, sps = PSS.next()
                kg = 0 if kt < 2 else 1 + (kt - 2) // 4
                self.MM(sp_[:, :n], KTz[j][:, kt * 128:(kt + 1) * 128], QT[:, c0:c0 + n], True, True,
                        r=[KTs[kg], QTs[g]], w=[sps])
                pt, pts = PT.next()
                self.ACT(pt[:, :n], sp_[:, :n], AF.Exp, r=[sps], w=[pts])
                it["pt"] = (pt, pts)

            def st_b(it, h=h):
                g, c0, n, j, kt = it["g"], it["c0"], it["n"], it["j"], it["kt"]
                if it["first"]:
                    cur["acc"] = ACC.next()
                (num, nums), (den, dens) = cur["acc"]
                pt, pts = it["pt"]
                self.MM(num[:, :n], VT[:, kt, :], pt[:, :n], it["first"], it["last"], r=[VTs, pts], w=[nums])
                self.MM(den[:, :n], self.ones_bf[:], pt[:, :n], it["first"], it["last"], r=[self.S_const, pts], w=[dens])
                if not it["last"]:
                    return
                rb, rbs = (R0, R1)[j]
                self.ACT(rb[:, :n], den[:, :n], AF.Ln, r=[dens], w=[rbs])
                self.ACT(rb[:, :n], rb[:, :n], AF.Exp, r=[rbs], w=[rbs], scale=-1.0)
                self.TT("dve", rb[:, :n], num[:, :n], rb[:, :n], ALU.mult, r=[nums, rbs], w=[rbs])
                if j == 0:
                    return
                ob, obs = OB
                self.STT("dve", ob[:, :n], R1[0][:, :n], neglam, R0[0][:, :n], ALU.mult, ALU.add, r=[R0[1], R1[1], S_l], w=[obs])
                sq, sqs = SQ
                self.ACT(sq[:, :n], ob[:, :n], AF.Square, r=[obs], w=[sqs])
                pp, pps = PN
                self.MM(pp[:, :n], self.ones_bf[:], sq[:, :n], True, True, r=[sqs, self.S_const], w=[pps])
                rs, rss = RS
                self.rstd_from_ps(pp, pps, n, 1.0 / 128, rs, rss)
                self.STT("dve", RT[:, c0:c0 + n], ob[:, :n], ngb[:, h:h + 1], rs[:, :n], ALU.mult, ALU.mult, r=[obs, rss, S_l], w=[RTs[g]])

            pipeline(items, st_a, st_b, look=2)
            self.wo_partial("da_w_o", [h], [RT], [RTs], WOb, WOs, PY)

    def mix_gla(self, i):
        self.phase()
        P = self.P
        win = self.dram["gla_w_in"].rearrange("(kc p) n -> p kc n", p=128)
        wo = self.dram["gla_w_o"].rearrange("(kc p) n -> p kc n", p=128)
        WL = self.abuf([128, KC, 32], BF16)
        wgu = self.abuf([128, 2, 512], F32)
        gng = self.abuf([128, KC], F32)
        onec = self.abuf([128, 1], F32)
        S_set = P.slot("glaset")
        kT = self.abuf([128, T], BF16)
        qT = self.abuf([128, T], BF16)
        ktok = self.abuf([128, 18, 128], BF16)
        VT = self.abuf([128, 18, 256], BF16)
        OH = self.abuf([128, 2, NLAT], F32)
        S32 = [self.abuf([128, 256], F32) for _ in range(2)]
        Sbf = [self.abuf([128, 256], BF16) for _ in range(2)]
        kTs, qTs, ktoks, VTs = P.slot("kT"), P.slot("qT"), P.slot("ktok"), P.slot("VT")
        OHs = [P.slot(f"oh{t}") for t in range(16)]
        S32s = [P.slot("s32f"), P.slot("s32b")]
        Sbfs = [P.slot("sbff"), P.slot("sbfb")]
        off_sub = self.ar_off
        PA = Rot([(self.ps[k], self.PS[k]) for k in range(4)])
        PO = Rot([(self.ps[4], self.PS[4]), (self.ps[5], self.PS[5])])
        PP = Rot([(self.ps[6], self.PS[6]), (self.ps[7], self.PS[7])])
        self.DMA("pool", WL[:], win[:, :, 1536:1568], "misc", w=[S_set])
        self.DMA("sp", wgu[0:33], self.dram["gla_wgu"], "spm1", w=[S_set])
        self.DMA("sp", gng[:], self.dram["gla_norm_g"], "spm2", w=[S_set])
        self.MEMSET("dve", onec[:], 1.0, w=[S_set])
        masks = [self.cmask[:, 0, 384:512], self.cmask[:, 1, 384:512]]
        orders = [list(range(18)), [1, 0] + list(range(17, 1, -1))]
        qscale = 128.0 ** -0.5
        for h in range(4):
            self.subphase(off_sub)
            L33 = Rot([(self.abuf([128, 128], F32), P.slot("l33")) for _ in range(4)])
            LA = Rot([(self.abuf([128, 128], F32), P.slot("la")) for _ in range(4)])
            EK = Rot([(self.abuf([128, 3, 128], F32), P.slot("ek")) for _ in range(3)])
            KQ = Rot([(self.abuf([128, 4, 128], BF16), P.slot("kq")) for _ in range(6)])
            SC = Rot([(self.abuf([128, 1], F32), P.slot("sc")) for _ in range(6)])
            UB = Rot([(self.abuf([128, 256], F32), P.slot("ub")) for _ in range(2)])
            s = self.ws_i % 2
            s2 = (self.ws_i + 1) % 2
            self.ws_i += 2
            WSb, WSs = self.WS[s], self.WSs[s]
            for (dc, sc_, wd_) in ((0, h * 128, 128), (128, 512 + h * 256, 256), (384, 1568 + h * 128, 128), (512, 2080 + h * 256, 256)):
                self.DMA("pool", WSb[:, :, dc:dc + wd_], win[:, :, sc_:sc_ + wd_], f"ws{s}", w=[WSs], nowaw=True)
            WOv = self.WS[s2][:, 0:3, :].rearrange("p a b -> p (a b)")[:, 0:2 * D].rearrange("p (a b) -> p a b", b=D)
            for q in range(2):
                self.DMA("pool", WOv[:, q, :], wo[:, 2 * h + q, :], f"ws{s2}", w=[self.WSs[s2]], nowaw=True)
            for it_ in L33.items:
                self.MEMSET("dve", it_[0][32:33, :], 1.0, w=[it_[1]])
            for d in range(2):
                self.MEMSET("dve", S32[d][:], 0.0, w=[S32s[d]])
                self.MEMSET("pool", Sbf[d][:], 0.0, w=[Sbfs[d]])
            for g, (c0, n, lc) in enumerate(GROUPS):
                for (dst, dsts, wc, sc) in ((kT, kTs, 0, 1.0), (qT, qTs, 384, qscale)):
                    pp, pps = PP.next()
                    for kc in range(KC):
                        self.MM(pp[:, :n], WSb[:, kc, wc:wc + 128], self.UT[:, kc, c0:c0 + n], kc == 0, kc == KC - 1,
                                r=[WSs, self.US[kc][g]], w=[pps])
                    self.ACT(dst[:, c0:c0 + n], pp[:, :n], AF.Identity, r=[pps], w=[dsts], scale=sc)
            self.proj_v_tok(WSb, WSs, 0, 128, ktok, ktoks, PP)
            self.proj_v_tok(WSb, WSs, 128, 256, VT, VTs, PP)
            touched = set()
            items = [dict(d=d, tt=orders[d][step]) for step in range(18) for d in range(2)]

            def st_a1(it, h=h):
                d, tt = it["d"], it["tt"]
                cols = slice(tt * 128, (tt + 1) * 128)
                gt = 0 if tt < 2 else 1 + (tt - 2) // 4
                pl, pls = PA.next()
                for kc in range(KC):
                    self.MM(pl[0:32, 0:128], WL[:, kc, :], self.UT[:, kc, cols], kc == 0, kc == KC - 1, r=[S_set, self.US[kc][gt]], w=[pls])
                l33, l33s = L33.next()
                self.CP("act", l33[0:32, :], pl[0:32, 0:128], r=[pls], w=[l33s])
                it["l33"] = (l33, l33s)

            def st_a2(it, h=h):
                d = it["d"]
                l33, l33s = it["l33"]
                pla, plas = PA.next()
                self.MM(pla[:, 0:128], l33[0:33, :], wgu[0:33, d, h * 128:(h + 1) * 128], True, True, r=[l33s, S_set], w=[plas])
                la, las = LA.next()
                self.ACT(la[:], pla[:, 0:128], AF.Exp, r=[plas], w=[las], scale=-1.0)
                self.ACT(la[:], la[:], AF.Ln, r=[las, S_set], w=[las], bias=onec[:, 0:1], scale=1.0)
                it["la"] = (la, las)

            def st_a3(it, h=h):
                d, tt = it["d"], it["tt"]
                cols = slice(tt * 128, (tt + 1) * 128)
                M = masks[d]
                la, las = it["la"]
                pbc, pbcs = PA.next()
                self.MM(pbc[:, 0:128], M, la[:], True, True, r=[las, self.S_const], w=[pbcs])
                self.MM(pbc[:, 128:256], la[:], M, True, True, r=[las, self.S_const], w=[pbcs])
                ek, eks = EK.next()
                self.ACT(ek[:, 0, :], pbc[:, 0:128], AF.Exp, r=[pbcs], w=[eks], scale=1.0 / 16)
                self.ACT(ek[:, 1, :], pbc[:, 128:256], AF.Exp, r=[pbcs], w=[eks], scale=1.0 / 16)
                self.ACT(ek[:, 2, :], pbc[:, 128:256], AF.Exp, r=[pbcs], w=[eks], scale=-1.0 / 16)
                kq, kqs = KQ.next()
                self.TT("dve", kq[:, 0, :], ktok[:, tt, :], ek[:, 0, :], ALU.mult, r=[ktoks, eks], w=[kqs])
                self.TT("pool", kq[:, 1, :], kT[:, cols], ek[:, 1, :], ALU.mult, r=[kTs, eks], w=[kqs])
                self.TT("dve", kq[:, 2, :], qT[:, cols], ek[:, 2, :], ALU.mult, r=[qTs, eks], w=[kqs])
                sc, scs = SC.next()
                te = 127 if d == 0 else 0
                self.CP("act", sc[:, 0:1], ek[:, 2, te:te + 1], r=[eks], w=[scs])
                it["sc"], it["kq"] = (sc, scs), (kq, kqs)

            def st_a4(it, h=h):
                d, tt = it["d"], it["tt"]
                if tt < 2:
                    return
                kq, kqs = it["kq"]
                pa, pas = PA.next()
                self.MM(pa[:, 0:128], kq[:, 1, :], kq[:, 2, :], True, True, r=[kqs], w=[pas])
                self.TT("dve", kq[:, 3, :], pa[:, 0:128], masks[d], ALU.mult, r=[pas, self.S_const, kqs], w=[kqs])

            def st_b(it, h=h):
                d, tt = it["d"], it["tt"]
                (sc, scs), (kq, kqs) = it["sc"], it["kq"]
                if tt >= 2:
                    po, pos = PO.next()
                    for ec in range(2):
                        self.MM(po[:, ec * 128:(ec + 1) * 128], VT[:, tt, ec * 128:(ec + 1) * 128], kq[:, 3, :], True, False,
                                r=[VTs, kqs], w=[pos])
                        self.MM(po[:, ec * 128:(ec + 1) * 128], Sbf[d][:, ec * 128:(ec + 1) * 128], kq[:, 2, :], False, True,
                                r=[Sbfs[d], kqs], w=[pos])
                    lt = tt - 2
                    ohv = OH[:, :, lt * 128:(lt + 1) * 128]
                    pov = po[:, 0:256].rearrange("p (a b) -> p a b", b=128)
                    if lt not in touched:
                        touched.add(lt)
                        self.CP("act", ohv, pov, r=[pos], w=[OHs[lt]])
                    else:
                        self.TT("dve", ohv, ohv, pov, ALU.add, r=[pos, OHs[lt]], w=[OHs[lt]])
                pst, psts = PP.next()
                self.MM(pst[:, 0:256], kq[:, 0, :], VT[:, tt, :], True, True, r=[kqs, VTs], w=[psts])
                ub, ubs = UB.next()
                self.TT("dve", ub[:], pst[:, 0:256], S32[d][:], ALU.add, r=[psts, S32s[d]], w=[ubs])
                self.ACT(S32[d][:], ub[:], AF.Identity, r=[ubs, scs], w=[S32s[d]], scale=sc[:, 0:1])
                self.TS("dve", Sbf[d][:], ub[:], sc[:, 0:1], None, ALU.mult, None, r=[ubs, scs], w=[Sbfs[d]])

            pipeline_n(items, [st_a1, st_a2, st_a3, st_a4, st_b])
            self.subphase(off_sub)
            SQ = Rot([(self.abuf([128, 512], BF16), P.slot("sq")) for _ in range(2)])
            RS = (self.abuf([128, 512], F32), P.slot("rs"))
            SR = Rot([(self.abuf([128, 512], F32), P.slot("sr")) for _ in range(2)])
            TM = Rot([(self.abuf([128, 512], F32), P.slot("tm")) for _ in range(2)])
            RTb = Rot([[(self.abuf([128, 512], BF16), P.slot("rt")) for _ in range(2)] for _ in range(2)])
            for g, (c0, n, lc) in enumerate(GROUPS):
                if lc:
                    continue
                l0 = c0 - NCTX
                ohs = OHs[l0 // 128:l0 // 128 + 4]
                pp, pps = PP.next()
                for cc in range(2):
                    sq, sqs = SQ.next()
                    self.ACT(sq[:, :n], OH[:, cc, l0:l0 + n], AF.Square, r=ohs, w=[sqs])
                    self.MM(pp[:, :n], self.ones_bf[:], sq[:, :n], cc == 0, cc == 1, r=[sqs, self.S_const], w=[pps])
                rs, rss = RS
                self.rstd_from_ps(pp, pps, n, 1.0 / 256, rs, rss)
                rt = RTb.next()
                for cc in range(2):
                    pr, prs = PP.next()
                    for kc in range(KC):
                        self.MM(pr[:, :n], WSb[:, kc, 512 + cc * 128:512 + (cc + 1) * 128], self.UT[:, kc, c0:c0 + n], kc == 0, kc == KC - 1,
                                r=[WSs, self.US[kc][g]], w=[prs])
                    sr, srs = SR.next()
                    self.ACT(sr[:, :n], pr[:, :n], AF.Silu, r=[prs], w=[srs])
                    tm, tms = TM.next()
                    self.STT("dve", tm[:, :n], OH[:, cc, l0:l0 + n], gng[:, 2 * h + cc:2 * h + cc + 1], rs[:, :n], ALU.mult, ALU.mult,
                             r=ohs + [S_set, rss], w=[tms])
                    self.TT("dve", rt[cc][0][:, :n], tm[:, :n], sr[:, :n], ALU.mult, r=[tms, srs], w=[rt[cc][1]])
                for c in range(KC):
                    py, pys = PO.next()
                    for cc in range(2):
                        self.MM(py[:, :n], WOv[:, cc, c * 128:(c + 1) * 128], rt[cc][0][:, :n], cc == 0, cc == 1,
                                r=[self.WSs[s2], rt[cc][1]], w=[pys])
                    self.STT("dve", self.hT[:, c, c0:c0 + n], py[:, :n], self.Gsc[:, 1, c, lc:lc + 1], self.hT[:, c, c0:c0 + n],
                             ALU.mult, ALU.add, r=[pys, self.S_sc, self.HS[c][g]], w=[self.HS[c][g]])

    def final(self):
        self.phase()
        P = self.P
        fin = []
        if self.debug_out:
            for c in range(KC):
                fin.append(self.DMA("sp", self.outT[c * 128:(c + 1) * 128, :], self.hT[:, c, :], f"out{c}", r=self.HS[c]))
            return fin
        SQ = Rot([(self.abuf([128, 512], BF16), P.slot("sq")) for _ in range(2)])
        RS = Rot([(self.abuf([128, 512], F32), P.slot("rs")) for _ in range(2)])
        OB = Rot([(self.abuf([128, 512], F32), P.slot("ob"), k) for k in range(4)])
        PSr = Rot([(self.ps[6], self.PS[6]), (self.ps[7], self.PS[7])])
        for g, (c0, n, lc) in enumerate(GROUPS):
            if lc:
                continue
            ssp, sss = PSr.next()
            for c in range(KC):
                sq, sqs = SQ.next()
                self.ACT(sq[:, :n], self.hT[:, c, c0:c0 + n], AF.Square, r=[self.HS[c][g]], w=[sqs])
                self.MM(ssp[:, :n], self.ones_bf[:], sq[:, :n], c == 0, c == KC - 1, r=[sqs, self.S_const], w=[sss])
            rs, rss = RS.next()
            self.rstd_from_ps(ssp, sss, n, 1.0 / D, rs, rss)
            for c in range(KC):
                ob, obs, k = OB.next()
                self.STT("dve", ob[:, :n], self.hT[:, c, c0:c0 + n], self.finalg[:, c:c + 1], rs[:, :n], ALU.mult, ALU.mult,
                         r=[self.HS[c][g], self.S_const, rss], w=[obs])
                fin.append(self.DMA("sp", self.outT[c * 128:(c + 1) * 128, c0 - NCTX:c0 - NCTX + n], ob[:, :n], f"out{k}", r=[obs]))
        return fin


def _fm(v):
    v = np.asarray(v, np.float32)
    lead = v.shape[:-1]
    return np.ascontiguousarray(np.moveaxis(v.reshape(lead + (KC, 128)), -1, 0))


def _na_bias(rpb):
    rpb = np.asarray(rpb, np.float32)
    specs = [(0, a) for a in range(0, 12, 2)] + [(1, a) for a in range(4, 20, 2)] + [(3, a) for a in range(20, 32, 2)]
    p = np.arange(128)
    x = np.arange(512)
    out = np.empty((16, 20, 128, 512), np.float32)
    for ti, (j, a) in enumerate(specs):
        kr = (a + p // 64)[:, None]
        kc = (p % 64)[:, None]
        qr = (8 * j + x // 64)[None, :]
        qc = (x % 64)[None, :]
        r0 = np.clip(qr - 4, 0, 24)
        w0 = np.clip(qc - 8, 0, 48)
        valid = (kr >= r0) & (kr < r0 + 8) & (kc >= w0) & (kc < w0 + 16)
        ro = np.clip(kr - qr + 7, 0, 14)
        co = np.clip(kc - qc, -15, 15) + 15
        out[:, ti] = np.where(valid[None], rpb[:, ro, co], np.float32(NEG))
    return out


def _consts():
    p = np.arange(128)[:, None]
    x = np.arange(896)[None, :]
    mf = (x - 384 >= p).astype(np.float32)
    mb = (x - 384 <= p).astype(np.float32)
    return np.ascontiguousarray(np.stack([mf, mb], axis=1)), np.eye(128, dtype=np.float32)


def _rope_tables():
    t = np.arange(NLAT)
    row = (t // 64).astype(np.float32)
    col = (t % 64).astype(np.float32)
    freqs = (np.float32(10000.0) ** (-np.arange(0, 32, 2, dtype=np.float32) / np.float32(32))).astype(np.float32)
    ar = row[:, None] * freqs
    ac = col[:, None] * freqs
    ang = np.concatenate([ar, ar, ac, ac], axis=-1)
    cos = np.cos(ang).astype(np.float32)
    sin = np.sin(ang).astype(np.float32)
    sgn = np.concatenate([-np.ones(16), np.ones(16), -np.ones(16), np.ones(16)]).astype(np.float32)
    tab = np.stack([cos.T, (sin * sgn[None, :]).T], axis=1)
    return np.ascontiguousarray(np.concatenate([tab, tab], axis=0).astype(np.float32))


def _prep(inputs, b):
    f32 = lambda a: np.ascontiguousarray(np.asarray(a, np.float32))
    m = {}
    m["xT"] = np.ascontiguousarray(np.concatenate([inputs["ctx"][b], inputs["x"][b]], axis=0).T.astype(np.float32))
    m["cs"] = np.ascontiguousarray(np.stack([_fm(inputs["c"][b]), _fm(inputs["c_ctx"])], axis=-1))
    return m


def _shared(inputs):
    f32 = lambda a: np.ascontiguousarray(np.asarray(a, np.float32))
    m = {}
    m["ada_w"] = f32(inputs["ada_w"])
    m["adab"] = np.ascontiguousarray(np.moveaxis(np.asarray(inputs["ada_b"], np.float32).reshape(4, 72, 128), -1, 0))
    m["normg"] = _fm(inputs["norm_g"])
    m["finalg"] = _fm(inputs["final_g"])
    m["ffn_w13"] = f32(inputs["ffn_w13"])
    m["ffn_w2"] = f32(inputs["ffn_w2"])
    m["na_w_kvq"] = f32(inputs["na_w_kvq"][0])
    m["na_w_o"] = f32(inputs["na_w_o"][0])
    m["na_bias"] = _na_bias(inputs["na_rpb"][0])
    m["cmask"], m["ident"] = _consts()
    wd = np.asarray(inputs["da_w_kvq"][0], np.float32)
    perm = np.concatenate([np.arange(16, 32), np.arange(0, 16), np.arange(48, 64), np.arange(32, 48)])
    pfull = (np.arange(0, D, 64)[:, None] + perm[None, :]).reshape(-1)
    m["da_w_kvq"] = f32(wd)
    m["da_w_rot"] = np.ascontiguousarray(np.concatenate([wd[:, 0:D][:, pfull], wd[:, 2 * D:3 * D][:, pfull]], axis=1))
    m["da_w_o"] = f32(inputs["da_w_o"][0])
    m["da_lam"] = np.ascontiguousarray(np.broadcast_to(np.asarray(inputs["da_lam"][0], np.float32)[None], (128, 4, 64)))
    m["da_norm_g"] = _fm(inputs["da_norm_g"][0])
    m["rope"] = _rope_tables()
    m["ml_w_in"] = f32(inputs["ml_w_in"][0])
    m["ml_w_o"] = f32(inputs["ml_w_o"][0])
    m["ml_gate_b"] = np.ascontiguousarray(np.broadcast_to(np.asarray(inputs["ml_gate_b"][0], np.float32)[None], (128, 16)))
    m["ml_conv"] = np.ascontiguousarray(np.transpose(np.asarray(inputs["ml_conv_w"][0], np.float32).reshape(3, 16, 128), (2, 1, 0)))
    m["ml_norm_g"] = _fm(inputs["ml_norm_g"][0])
    m["gla_w_in"] = f32(inputs["gla_w_in"][0])
    m["gla_w_o"] = f32(inputs["gla_w_o"][0])
    gu = np.asarray(inputs["gla_w_gate_up"][0], np.float32)
    gb = np.asarray(inputs["gla_b_gate"][0], np.float32)
    wgu = np.zeros((33, 2, 512), np.float32)
    wgu[0:16, 0] = gu[0]
    wgu[16:32, 1] = gu[1]
    wgu[32] = gb
    m["gla_wgu"] = wgu
    m["gla_norm_g"] = _fm(inputs["gla_norm_g"][0])
    return m


def run(inputs, stop=None, debug_out=False, ncores=8):
    k = Kern(stop=stop, debug_out=debug_out)
    nc = k.build()
    shared = _shared(inputs)
    in_maps = []
    for b in range(ncores):
        m = dict(shared)
        m.update(_prep(inputs, b))
        in_maps.append({n: m[n] for n in k.dram})
    res = run_bass_kernel_spmd(nc, in_maps, core_ids=list(range(ncores)))
    return [r["outT"] for r in res.results], k


def kernel(**inputs):
    outs, _ = run(inputs)
    return np.stack([np.ascontiguousarray(o.T) for o in outs], axis=0).astype(np.float32)
```
